# Optimizing a Trainium2 kernel written in Bass

```python
import jax
import jax.numpy as jnp
from jax import lax
import numpy as np

D_MODEL = 1024
BATCH = 8
SEQ = 8192
DEPTH = 2

RWKV_HEADS = 8
RWKV_HEAD_DIM = 64
RWKV_W = RWKV_HEADS * RWKV_HEAD_DIM
DECAY_LORA = 64
ICLR_LORA = 64
GATE_LORA = 128
VRES_LORA = 32
GN_EPS = 64e-5
GMLP_GROUPS = 4
GMLP_W = 512
GMLP_GROUP_DIM = GMLP_W // GMLP_GROUPS
CHUNK = 128
LRU_HEADS = 8
LRU_W = 512
LRU_HEAD_DIM = LRU_W // LRU_HEADS
CONV_WIDTH = 4
LRU_C = 8.0
N_BRANCH = 3
BRANCH_W = 512
D_FF = 2816
LN_EPS = 1e-5
ALPHA = (2 * DEPTH) ** 0.25
BETA = (8 * DEPTH) ** -0.25

OFF_R = 0
OFF_K = OFF_R + RWKV_W
OFF_V = OFF_K + RWKV_W
OFF_W = OFF_V + RWKV_W
OFF_A = OFF_W + DECAY_LORA
OFF_G = OFF_A + ICLR_LORA
RWKV_COLS = OFF_G + GATE_LORA
OFF_GU = RWKV_COLS
OFF_GV = OFF_GU + GMLP_W
OFF_LX = OFF_GV + GMLP_W
OFF_LY = OFF_LX + LRU_W
OFF_GATE = OFF_LY + LRU_W
N_IN = OFF_GATE + N_BRANCH * D_MODEL

kernel_name = 'hybrid_rwkv7_gmlp_rglru_deepnorm'


def layer_norm(x, g, b, eps=LN_EPS):
    xf = x.astype(jnp.float32)
    mu = jnp.mean(xf, -1, keepdims=True)
    var = jnp.mean(jnp.square(xf - mu), -1, keepdims=True)
    y = (xf - mu) * lax.rsqrt(var + eps) * g.astype(jnp.float32) + b.astype(jnp.float32)
    return y.astype(x.dtype)


def swiglu(x, w1, w3, w2):
    return (jax.nn.silu(x @ w1) * (x @ w3)) @ w2


def token_shift(z):
    return jnp.pad(z, ((0, 0), (1, 0), (0, 0)))[:, :-1]


def rwkv7_wkv(r, w, k, v, kk, a):
    B, _, H, N = r.shape
    xs = tuple(jnp.swapaxes(t, 0, 1) for t in (r, w, k, v, kk, a))

    def step(state, inp):
        r_t, w_t, k_t, v_t, kk_t, a_t = inp
        sa = jnp.einsum('bhvk,bhk->bhv', state, -kk_t)
        state = (state * w_t[:, :, None, :]
                 + sa[..., None] * (kk_t * a_t)[:, :, None, :]
                 + v_t[..., None] * k_t[:, :, None, :])
        return state, jnp.einsum('bhvk,bhk->bhv', state, r_t)

    state0 = jnp.zeros((B, H, N, N), jnp.float32)
    _, out = lax.scan(step, state0, xs)
    return jnp.swapaxes(out, 0, 1)


def rwkv7_branch(z, v_first, w0, w2, a0, a2, g2, k_k, k_a, r_k, gn_g, gn_b, vres):
    B, S, _ = z.shape
    f32 = jnp.float32
    heads = lambda t: t.reshape(B, S, RWKV_HEADS, RWKV_HEAD_DIM).astype(f32)
    r = z[..., OFF_R:OFF_K]
    k = z[..., OFF_K:OFF_V]
    v = z[..., OFF_V:OFF_W]
    zw = z[..., OFF_W:OFF_A]
    za = z[..., OFF_A:OFF_G]
    zg = z[..., OFF_G:RWKV_COLS]
    v_raw = v
    if vres is not None:
        v0, v1, v2 = vres
        v = v + (v_first - v) * jax.nn.sigmoid(v0 + (v @ v1) @ v2)
    w_log = -jax.nn.softplus(-(w0 + jnp.tanh(zw) @ w2)) - 0.5
    decay = jnp.exp(-jnp.exp(w_log.astype(f32)))
    a = jax.nn.sigmoid(a0 + za @ a2)
    g = jax.nn.sigmoid(zg) @ g2
    kk = heads(k * k_k)
    kk = kk / jnp.maximum(jnp.linalg.norm(kk, axis=-1, keepdims=True), 1e-12)
    k = k * (1.0 + (a - 1.0) * k_a)
    rh, kh, vh = heads(r), heads(k), heads(v)
    o = rwkv7_wkv(rh, heads(decay), kh, vh, kk, heads(a))
    mu = jnp.mean(o, -1, keepdims=True)
    var = jnp.mean(jnp.square(o - mu), -1, keepdims=True)
    o = ((o - mu) * lax.rsqrt(var + GN_EPS)).reshape(B, S, RWKV_W) * gn_g + gn_b
    bonus = jnp.sum(rh * kh * r_k, -1, keepdims=True) * vh
    o = (o + bonus.reshape(B, S, RWKV_W)) * g
    return o.astype(z.dtype), v_raw


def gmlp_branch(zu, zv, ln_g, ln_b, ws, sb):
    B, S, _ = zu.shape
    u = jax.nn.gelu(zu)
    v = layer_norm(jax.nn.gelu(zv), ln_g, ln_b)
    vc = v.reshape(B, S // CHUNK, CHUNK, GMLP_GROUPS, GMLP_GROUP_DIM)
    mask = jnp.tril(jnp.ones((CHUNK, CHUNK), dtype=bool))
    wm = jnp.where(mask[None], ws, jnp.zeros_like(ws))
    s = jnp.einsum('gts,bcsgd->bctgd', wm, vc) + jnp.swapaxes(sb, 0, 1)[None, None, :, :, None]
    return u * s.reshape(B, S, GMLP_W)


def rglru_branch(zx, zy, conv_w, conv_b, wa, ba, wx, bx, lam):
    B, S, _ = zx.shape
    f32 = jnp.float32
    y = jax.nn.gelu(zy)
    xp = jnp.pad(zx, ((0, 0), (CONV_WIDTH - 1, 0), (0, 0)))
    xc = conv_b + xp[:, 0:S] * conv_w[0]
    for j in range(1, CONV_WIDTH):
        xc = xc + xp[:, j:j + S] * conv_w[j]
    xh = xc.reshape(B, S, LRU_HEADS, LRU_HEAD_DIM)
    rg = jax.nn.sigmoid(jnp.einsum('bshi,hij->bshj', xh, wa) + ba).reshape(B, S, LRU_W)
    ig = jax.nn.sigmoid(jnp.einsum('bshi,hij->bshj', xh, wx) + bx).reshape(B, S, LRU_W)
    log_a = -LRU_C * rg.astype(f32) * jax.nn.softplus(-lam.astype(f32))
    a = jnp.exp(log_a)
    b = jnp.sqrt(-jnp.expm1(2.0 * log_a)) * (ig * xc).astype(f32)

    def combine(left, right):
        a_l, b_l = left
        a_r, b_r = right
        return a_l * a_r, a_r * b_l + b_r

    _, h = lax.associative_scan(combine, (a, b), axis=1)
    return y * h.astype(zx.dtype)


def hybrid_mixer(x, w_in, gate_b, p_branch, w_out, mu, v_first, rwkv_p, vres, gmlp_p, lru_p):
    B, S, _ = x.shape
    z = x @ w_in
    zr = z[..., :RWKV_COLS]
    zr = zr + mu * (token_shift(zr) - zr)
    o_rwkv, v_raw = rwkv7_branch(zr, v_first, *rwkv_p, vres)
    o_gmlp = gmlp_branch(z[..., OFF_GU:OFF_GV], z[..., OFF_GV:OFF_LX], *gmlp_p)
    o_lru = rglru_branch(z[..., OFF_LX:OFF_LY], z[..., OFF_LY:OFF_GATE], *lru_p)
    gates = jax.nn.sigmoid(z[..., OFF_GATE:].reshape(B, S, N_BRANCH, D_MODEL) + gate_b)
    merged = (gates[:, :, 0] * (o_rwkv @ p_branch[0])
              + gates[:, :, 1] * (o_gmlp @ p_branch[1])
              + gates[:, :, 2] * (o_lru @ p_branch[2]))
    return merged @ w_out, v_raw


def setup_inputs(seed: int = 0) -> dict:
    key = jax.random.key(seed)
    ks = iter(jax.random.split(key, 48))
    f32 = jnp.float32

    def nrm(shape, scale):
        return jax.random.normal(next(ks), shape, f32) * scale

    x = nrm((BATCH, SEQ, D_MODEL), 1.0)
    ln_g = 1.0 + nrm((DEPTH, 3, D_MODEL), 0.02)
    ln_b = nrm((DEPTH, 3, D_MODEL), 0.02)
    ffn_w1 = nrm((DEPTH, 2, D_MODEL, D_FF), D_MODEL ** -0.5)
    ffn_w3 = nrm((DEPTH, 2, D_MODEL, D_FF), D_MODEL ** -0.5)
    ffn_w2 = nrm((DEPTH, 2, D_FF, D_MODEL), BETA * D_FF ** -0.5)
    w_in = nrm((DEPTH, D_MODEL, N_IN), D_MODEL ** -0.5)
    gate_b = nrm((DEPTH, N_BRANCH, D_MODEL), 0.02)
    p_branch = nrm((DEPTH, N_BRANCH, BRANCH_W, D_MODEL), BRANCH_W ** -0.5)
    w_out = nrm((DEPTH, D_MODEL, D_MODEL), BETA * D_MODEL ** -0.5)
    rwkv_mu = jax.random.uniform(next(ks), (DEPTH, RWKV_COLS), f32)
    rwkv_w0 = jnp.tile(jnp.linspace(-6.0, -1.0, RWKV_HEAD_DIM, dtype=f32), RWKV_HEADS)[None, :] + nrm((DEPTH, RWKV_W), 0.1)
    rwkv_w2 = nrm((DEPTH, DECAY_LORA, RWKV_W), 0.1 * DECAY_LORA ** -0.5)
    rwkv_a0 = nrm((DEPTH, RWKV_W), 0.1)
    rwkv_a2 = nrm((DEPTH, ICLR_LORA, RWKV_W), 0.5 * ICLR_LORA ** -0.5)
    rwkv_g2 = nrm((DEPTH, GATE_LORA, RWKV_W), GATE_LORA ** -0.5)
    rwkv_k_k = 0.85 + nrm((DEPTH, RWKV_W), 0.05)
    rwkv_k_a = 1.0 + nrm((DEPTH, RWKV_W), 0.05)
    rwkv_r_k = nrm((DEPTH, RWKV_HEADS, RWKV_HEAD_DIM), 0.1)
    rwkv_gn_g = 1.0 + nrm((DEPTH, RWKV_W), 0.02)
    rwkv_gn_b = nrm((DEPTH, RWKV_W), 0.02)
    rwkv_v0 = 1.0 + nrm((DEPTH - 1, RWKV_W), 0.1)
    rwkv_v1 = nrm((DEPTH - 1, RWKV_W, VRES_LORA), RWKV_W ** -0.5)
    rwkv_v2 = nrm((DEPTH - 1, VRES_LORA, RWKV_W), 0.5 * VRES_LORA ** -0.5)
    gmlp_ln_g = 1.0 + nrm((DEPTH, GMLP_W), 0.02)
    gmlp_ln_b = nrm((DEPTH, GMLP_W), 0.02)
    gmlp_ws = nrm((DEPTH, GMLP_GROUPS, CHUNK, CHUNK), CHUNK ** -0.5)
    gmlp_sb = 1.0 + nrm((DEPTH, GMLP_GROUPS, CHUNK), 0.1)
    lru_conv_w = nrm((DEPTH, CONV_WIDTH, LRU_W), CONV_WIDTH ** -0.5)
    lru_conv_b = nrm((DEPTH, LRU_W), 0.02)
    lru_wa = nrm((DEPTH, LRU_HEADS, LRU_HEAD_DIM, LRU_HEAD_DIM), LRU_HEAD_DIM ** -0.5)
    lru_ba = nrm((DEPTH, LRU_HEADS, LRU_HEAD_DIM), 0.02)
    lru_wx = nrm((DEPTH, LRU_HEADS, LRU_HEAD_DIM, LRU_HEAD_DIM), LRU_HEAD_DIM ** -0.5)
    lru_bx = nrm((DEPTH, LRU_HEADS, LRU_HEAD_DIM), 0.02)
    u = jax.random.uniform(next(ks), (DEPTH, LRU_W), f32, 0.9, 0.999)
    s = u ** (1.0 / LRU_C)
    lru_lam = jnp.log(s) - jnp.log1p(-s)
    return {'x': x, 'ln_g': ln_g, 'ln_b': ln_b, 'ffn_w1': ffn_w1, 'ffn_w3': ffn_w3, 'ffn_w2': ffn_w2,
            'w_in': w_in, 'gate_b': gate_b, 'p_branch': p_branch, 'w_out': w_out,
            'rwkv_mu': rwkv_mu, 'rwkv_w0': rwkv_w0, 'rwkv_w2': rwkv_w2, 'rwkv_a0': rwkv_a0, 'rwkv_a2': rwkv_a2,
            'rwkv_g2': rwkv_g2, 'rwkv_k_k': rwkv_k_k, 'rwkv_k_a': rwkv_k_a, 'rwkv_r_k': rwkv_r_k,
            'rwkv_gn_g': rwkv_gn_g, 'rwkv_gn_b': rwkv_gn_b, 'rwkv_v0': rwkv_v0, 'rwkv_v1': rwkv_v1, 'rwkv_v2': rwkv_v2,
            'gmlp_ln_g': gmlp_ln_g, 'gmlp_ln_b': gmlp_ln_b, 'gmlp_ws': gmlp_ws, 'gmlp_sb': gmlp_sb,
            'lru_conv_w': lru_conv_w, 'lru_conv_b': lru_conv_b, 'lru_wa': lru_wa, 'lru_ba': lru_ba,
            'lru_wx': lru_wx, 'lru_bx': lru_bx, 'lru_lam': lru_lam}


def reference(x, ln_g, ln_b, ffn_w1, ffn_w3, ffn_w2, w_in, gate_b, p_branch, w_out,
              rwkv_mu, rwkv_w0, rwkv_w2, rwkv_a0, rwkv_a2, rwkv_g2, rwkv_k_k, rwkv_k_a, rwkv_r_k,
              rwkv_gn_g, rwkv_gn_b, rwkv_v0, rwkv_v1, rwkv_v2,
              gmlp_ln_g, gmlp_ln_b, gmlp_ws, gmlp_sb,
              lru_conv_w, lru_conv_b, lru_wa, lru_ba, lru_wx, lru_bx, lru_lam):
    v_first = None
    for l in range(DEPTH):
        x = layer_norm(ALPHA * x + 0.5 * swiglu(x, ffn_w1[l, 0], ffn_w3[l, 0], ffn_w2[l, 0]), ln_g[l, 0], ln_b[l, 0])
        vres = None if l == 0 else (rwkv_v0[l - 1], rwkv_v1[l - 1], rwkv_v2[l - 1])
        rwkv_p = (rwkv_w0[l], rwkv_w2[l], rwkv_a0[l], rwkv_a2[l], rwkv_g2[l], rwkv_k_k[l], rwkv_k_a[l],
                  rwkv_r_k[l], rwkv_gn_g[l], rwkv_gn_b[l])
        gmlp_p = (gmlp_ln_g[l], gmlp_ln_b[l], gmlp_ws[l], gmlp_sb[l])
        lru_p = (lru_conv_w[l], lru_conv_b[l], lru_wa[l], lru_ba[l], lru_wx[l], lru_bx[l], lru_lam[l])
        y, v_raw = hybrid_mixer(x, w_in[l], gate_b[l], p_branch[l], w_out[l], rwkv_mu[l], v_first,
                                rwkv_p, vres, gmlp_p, lru_p)
        if l == 0:
            v_first = v_raw
        x = layer_norm(ALPHA * x + y, ln_g[l, 1], ln_b[l, 1])
        x = layer_norm(ALPHA * x + 0.5 * swiglu(x, ffn_w1[l, 1], ffn_w3[l, 1], ffn_w2[l, 1]), ln_g[l, 2], ln_b[l, 2])
    return x
```

```python
from contextlib import ExitStack
import numpy as np
import concourse.bass as bass
import concourse.mybir as mybir
from concourse.bass_utils import run_bass_kernel_spmd

F32 = mybir.dt.float32
BF16 = mybir.dt.bfloat16
AF = mybir.ActivationFunctionType
ALU = mybir.AluOpType
AX = mybir.AxisListType

ENGS = ("sync", "scalar", "gpsimd", "vector", "tensor")


class Buf:
    __slots__ = ("name", "last_w", "readers")

    def __init__(self, name=""):
        self.name = name
        self.last_w = None
        self.readers = []


class Op:
    __slots__ = ("eng", "fn", "deps", "is_dma", "sig", "idx", "eidx", "dsem", "waits", "need_sig")


class Sched:
    def __init__(self, nc):
        self.nc = nc
        self.ops = []
        self.eng_ops = {e: [] for e in ENGS}

    def op(self, eng, fn, reads=(), writes=(), dma=False, dsem=None):
        o = Op()
        o.eng = eng
        o.fn = fn
        o.is_dma = dma
        o.dsem = dsem
        o.deps = []
        o.sig = None
        o.need_sig = dma
        o.waits = []
        o.idx = len(self.ops)
        o.eidx = len(self.eng_ops[eng])
        deps = {}
        for r in reads:
            if r.last_w is not None:
                deps[r.last_w.idx] = (r.last_w, "raw")
        for w in writes:
            if w.last_w is not None and w.last_w.idx not in deps:
                deps[w.last_w.idx] = (w.last_w, "waw")
            for rd in w.readers:
                if rd.idx not in deps:
                    deps[rd.idx] = (rd, "war")
        for r in reads:
            r.readers.append(o)
        for w in writes:
            w.last_w = o
            w.readers = []
        for p, kind in deps.values():
            if p is o:
                continue
            if p.eng == eng and not p.is_dma and not dma:
                if eng == "tensor":
                    continue
            o.deps.append(p)
            p.need_sig = True
        self.ops.append(o)
        self.eng_ops[eng].append(o)
        return o

    def emit(self, final_bufs=()):
        nc = self.nc
        final_ops = []
        for b in final_bufs:
            if b.last_w is not None:
                final_ops.append(b.last_w)
            final_ops.extend(b.readers)
        for p in final_ops:
            p.need_sig = True
        counter = {}
        waited = {e: {} for e in ENGS}
        for o in self.ops:
            for p in o.deps:
                key = p.sig[0]
                val = counter[key] if p.is_dma else p.sig[1]
                if waited[o.eng].get(key, 0) >= val:
                    continue
                waited[o.eng][key] = val
                o.waits.append((key, val))
            if o.need_sig:
                key = ("dma", o.dsem) if o.is_dma else ("eng", o.eng)
                counter[key] = counter.get(key, 0) + (16 if o.is_dma else 1)
                o.sig = (key, counter[key])
        fin = {}
        for p in final_ops:
            key = p.sig[0]
            val = counter[key] if p.is_dma else p.sig[1]
            fin[key] = max(fin.get(key, 0), val)
        with ExitStack() as es:
            sems = {}
            for i, key in enumerate(counter):
                sems[key] = es.enter_context(nc.semaphore("s%d" % i))
            block = es.enter_context(nc.Block())
            for e in ENGS:
                ops = self.eng_ops[e]
                is_last = e == "sync"

                def body(eng, ops=ops, is_last=is_last):
                    for o in ops:
                        for key, val in o.waits:
                            eng.wait_ge(sems[key], val)
                        inst = o.fn(eng)
                        if o.need_sig:
                            inst.then_inc(sems[o.sig[0]], 16 if o.is_dma else 1)
                    if is_last:
                        for key, val in fin.items():
                            eng.wait_ge(sems[key], val)

                getattr(block, e)(body)
        return counter


class Tile:
    __slots__ = ("ap", "b")

    def __init__(self, ap, b):
        self.ap = ap
        self.b = b


class Pool:
    def __init__(self, tiles):
        self.free = list(tiles)

    def get(self):
        return self.free.pop(0)

    def put(self, *ts):
        for t in ts:
            self.free.append(t)


D = 1024
KC = 8
DFF = 2816
MC = 22
NIN = 6912
T = 512
DEPTH = 2
ALPHA = (2 * DEPTH) ** 0.25
C_FFN = 0.5 / ALPHA
C_MIX = 1.0 / ALPHA
LN_EPS = 1e-5 / (ALPHA * ALPHA)
GN_EPS = 64e-5
NEG_E = -float(np.exp(-0.5))
OFF_GATE = 3840

PV = {}
_o = 0
for _n, _w in [("mu", 14), ("ln_g", 24), ("ln_b", 24), ("gate_b", 24), ("w0", 4), ("a0", 4), ("k_k", 4), ("k_a", 4),
               ("r_k", 4), ("gn_g", 4), ("gn_b", 4), ("v0", 4), ("cw", 16), ("cb", 4), ("ba", 4), ("bx", 4), ("lam", 4),
               ("omu", 14), ("oka", 4), ("clam", 4), ("clam2", 4)]:
    PV[_n] = _o
    _o += _w
NPV = _o
PM = {"wa": 0, "g2": 512, "v1": 1024, "v2": 1152, "ws": 1664, "lwa": 2176, "lwx": 2688}
NPM = 3200
CS = {"ident": 0, "iu": 128, "iuT": 256, "iue": 384, "blk": 512, "reset": 640}
NCS = 640 + 512


def host_consts():
    c = np.zeros((128, NCS), np.float32)
    i = np.arange(128)
    c[:, 0:128] = np.eye(128)
    c[:, 128:256] = (i[:, None] < i[None, :])
    c[:, 256:384] = (i[None, :] < i[:, None])
    c[:, 384:512] = (i[:, None] <= i[None, :])
    c[:, 512:640] = (i[:, None] // 64 == i[None, :] // 64)
    r = np.ones(512, np.float32)
    r[0::128] = 0.0
    c[:, 640:1152] = r[None, :]
    return c


def host_params(inp):
    pv = np.zeros((2, 128, NPV), np.float32)
    pm = np.zeros((2, 128, NPM), np.float32)
    pb = np.zeros((2, 128, 1536), np.float32)
    col = lambda v: np.ascontiguousarray(np.asarray(v, np.float32).reshape(-1, 128).T)
    for l in range(2):
        def put(name, v):
            a = col(v)
            pv[l, :, PV[name]:PV[name] + a.shape[1]] = a
        put("mu", inp["rwkv_mu"][l])
        put("ln_g", inp["ln_g"][l])
        put("ln_b", inp["ln_b"][l])
        put("gate_b", inp["gate_b"][l])
        put("w0", inp["rwkv_w0"][l])
        put("a0", inp["rwkv_a0"][l])
        put("k_k", inp["rwkv_k_k"][l])
        put("k_a", inp["rwkv_k_a"][l])
        put("r_k", inp["rwkv_r_k"][l])
        put("gn_g", inp["rwkv_gn_g"][l])
        put("gn_b", inp["rwkv_gn_b"][l])
        if l == 1:
            put("v0", inp["rwkv_v0"][0])
        put("cw", inp["lru_conv_w"][l])
        put("cb", inp["lru_conv_b"][l])
        put("ba", inp["lru_ba"][l])
        put("bx", inp["lru_bx"][l])
        put("lam", inp["lru_lam"][l])
        pm[l, 0:64, 0:512] = inp["rwkv_w2"][l]
        pm[l, 64:128, 0:512] = inp["rwkv_a2"][l]
        pm[l, :, 512:1024] = inp["rwkv_g2"][l]
        if l == 1:
            pm[l, :, 1024:1152] = np.asarray(inp["rwkv_v1"][0]).reshape(4, 128, 32).transpose(1, 0, 2).reshape(128, 128)
            pm[l, 0:32, 1152:1664] = inp["rwkv_v2"][0]
        pm[l, :, 1664:2176] = np.asarray(inp["gmlp_ws"][l]).transpose(2, 0, 1).reshape(128, 512)
        for nm, key in (("lwa", "lru_wa"), ("lwx", "lru_wx")):
            wv = np.asarray(inp[key][l])
            for c in range(4):
                for h in range(2):
                    pm[l, 64 * h:64 * h + 64, PM[nm] + c * 128 + 64 * h:PM[nm] + c * 128 + 64 * h + 64] = wv[2 * c + h]
        pb[l, :, 0:512] = np.asarray(inp["gmlp_ln_g"][l])[None, :]
        pb[l, :, 512:1024] = np.asarray(inp["gmlp_ln_b"][l])[None, :]
        pb[l, :, 1024:1536] = np.asarray(inp["gmlp_sb"][l]).reshape(-1)[None, :]
    return pv, pm, pb


def build(SEQ):
    NT = SEQ // T
    nc = bass.Bass("TRN2", target_bir_lowering=False)
    dt_in = lambda n, s: nc.dram_tensor(n, s, F32, kind="ExternalInput").ap()
    x_d = dt_in("x", [SEQ, D])
    w1_d = dt_in("ffn_w1", [2, 2, D, DFF])
    w3_d = dt_in("ffn_w3", [2, 2, D, DFF])
    w2_d = dt_in("ffn_w2", [2, 2, DFF, D])
    win_d = dt_in("w_in", [2, D, NIN])
    pb_d = dt_in("p_branch", [2, 3, 512, D])
    wo_d = dt_in("w_out", [2, D, D])
    pv_d = dt_in("pvec", [2, 128, NPV])
    pm_d = dt_in("pmat", [2, 128, NPM])
    pbc_d = dt_in("pbc", [2, 128, 1536])
    cs_d = dt_in("cst", [128, NCS])
    out_d = nc.dram_tensor("out", [SEQ, D], F32, kind="ExternalOutput").ap()
    import os
    DBG = os.environ.get("KDBG")
    ACTCOPY = os.environ.get("KACT", "")
    NTAP = 200
    dbg_d = nc.dram_tensor("dbg", [NTAP, 128, T], F32, kind="ExternalOutput").ap() if DBG else None
    TAPS.clear()
    tap_bufs = []
    tap_on = [False]
    sc = lambda n, s: nc.dram_tensor(n, s, BF16, kind="Internal").ap()
    w1_s = sc("w1_s", [2, 2, D, DFF])
    w3_s = sc("w3_s", [2, 2, D, DFF])
    w2_s = sc("w2_s", [2, 2, DFF, D])
    win_s = sc("win_s", [2, D, NIN])
    pb_s = sc("pb_s", [2, 3, 512, D])
    wo_s = sc("wo_s", [2, D, D])

    S = Sched(nc)
    es = ExitStack()
    sbuf = lambda n, s, d: es.enter_context(nc.sbuf_tensor(n, s, d))

    N32, N16, NS = 40, 36, 5
    p32_t = sbuf("p32", [128, N32, T], F32)
    p16_t = sbuf("p16", [128, N16, T], BF16)
    P32 = Pool([Tile(p32_t[:, i, :], Buf("p32_%d" % i)) for i in range(N32)])
    P16 = Pool([Tile(p16_t[:, i, :], Buf("p16_%d" % i)) for i in range(N16)])
    ws_t = sbuf("wslots", [128, NS, 4096], BF16)
    ws_b = [Buf("ws%d" % i) for i in range(NS)]
    ws_i = [0]
    ps_t = [es.enter_context(nc.psum_tensor("ps%d" % i, [128, T], F32)) for i in range(8)]
    PS = [Tile(ps_t[i][:], Buf("ps%d" % i)) for i in range(8)]
    ps_i = [0]

    ps_held = set()

    def psget(hold=False):
        while (ps_i[0] % 8) in ps_held:
            ps_i[0] += 1
        i = ps_i[0] % 8
        ps_i[0] += 1
        if hold:
            ps_held.add(i)
        return PS[i]

    def psrel(*ts_):
        for t_ in ts_:
            ps_held.discard(PS.index(t_))

    cst = sbuf("cst32", [128, NCS], F32)
    cst_b = Buf("cst")
    cbf = sbuf("cstbf", [128, 640], BF16)
    cbf_b = Buf("cbf")
    mNN = sbuf("mNN", [128, 512], BF16)
    mQ = sbuf("mQ", [128, 384], BF16)
    mBR = sbuf("mBR", [128, 512], BF16)
    msk_b = Buf("msk")
    onesm = sbuf("onesm", [128, 128], BF16)
    blkmean = sbuf("blkmean", [128, 128], BF16)
    zeros16 = sbuf("zeros16", [128, 128], BF16)
    pv = sbuf("pv", [128, 2, NPV], F32)
    pv_b = Buf("pv")
    pm = sbuf("pm", [128, 2, NPM], BF16)
    pm_b = Buf("pm")
    pbc = sbuf("pbc_sb", [128, 2, 1536], F32)
    pbc_b = Buf("pbc")
    zprev = sbuf("zprev", [128, 2, 14], F32)
    zprev_b = [[Buf() for _ in range(14)] for _ in range(2)]
    Sbf = sbuf("Sbf", [128, 2, 4, 128], BF16)
    Sbf_b = [[Buf() for _ in range(4)] for _ in range(2)]
    hprev = sbuf("hprev", [128, 2, 4], F32)
    hprev_b = [[Buf() for _ in range(4)] for _ in range(2)]
    halo = sbuf("halo", [128, 2, 4, 3], F32)
    halo_b = [[Buf() for _ in range(4)] for _ in range(2)]
    zxb = sbuf("zxb", [128, 4, 515], F32)
    zxb_b = [Buf() for _ in range(4)]
    small = sbuf("small", [128, 16], F32)
    small_b = Buf("small")

    ident32 = cst[:, 0:128]
    identb = cbf[:, 0:128]
    blkb = cbf[:, 512:640]
    iueb = cbf[:, 384:512]

    def bl(xs):
        return [x.b if isinstance(x, Tile) else x for x in xs]

    def act(out, in_, func, r, w, bias=None, scale=None):
        if func == AF.Identity:
            if bias is None and scale is None:
                if ACTCOPY == "copy":
                    S.op("scalar", lambda e: e.copy(out=out, in_=in_), bl(r), bl(w))
                elif ACTCOPY == "lrelu":
                    S.op("scalar", lambda e: e.activation(out=out, in_=in_, func=AF.Lrelu, alpha=1.0), bl(r), bl(w))
                elif ACTCOPY == "relu":
                    S.op("scalar", lambda e: e.activation(out=out, in_=in_, func=AF.Relu), bl(r), bl(w))
                else:
                    S.op("vector", lambda e: e.tensor_copy(out=out, in_=in_), bl(r), bl(w))
            else:
                S.op("vector", lambda e: e.tensor_scalar(out=out, in0=in_, scalar1=scale, scalar2=bias, op0=ALU.mult, op1=ALU.add), bl(r), bl(w))
            return
        kw = {}
        if bias is not None:
            kw["bias"] = bias
        if scale is not None:
            kw["scale"] = scale
        S.op("scalar", lambda e: e.activation(out=out, in_=in_, func=func, **kw), bl(r), bl(w))

    def tt(eng, out, in0, in1, op, r, w):
        S.op(eng, lambda e: e.tensor_tensor(out=out, in0=in0, in1=in1, op=op), bl(r), bl(w))

    def stt(out, in0, scalar, in1, op0, op1, r, w):
        S.op("vector", lambda e: e.scalar_tensor_tensor(out=out, in0=in0, scalar=scalar, in1=in1, op0=op0, op1=op1), bl(r), bl(w))

    def ts(out, in0, s1, s2, op0, op1, r, w, eng="vector"):
        S.op(eng, lambda e: e.tensor_scalar(out=out, in0=in0, scalar1=s1, scalar2=s2, op0=op0, op1=op1), bl(r), bl(w))

    def ts1(out, in0, s1, op0, r, w, eng="vector"):
        S.op(eng, lambda e: e.tensor_single_scalar(out=out, in_=in0, scalar=s1, op=op0), bl(r), bl(w))

    def cp(eng, out, in_, r, w):
        if eng == "scalar":
            S.op(eng, lambda e: e.copy(out=out, in_=in_), bl(r), bl(w))
        else:
            S.op(eng, lambda e: e.tensor_copy(out=out, in_=in_), bl(r), bl(w))

    def mm(out, lhsT, rhs, start, stop, r, w, tp=None):
        if tp is None:
            S.op("tensor", lambda e: e.matmul(out, lhsT=lhsT, rhs=rhs, start=start, stop=stop), bl(r), bl(w))
        else:
            S.op("tensor", lambda e: e.matmul(out, lhsT=lhsT, rhs=rhs, start=start, stop=stop, tile_position=tp), bl(r), bl(w))

    def tr(out, in_, ident, r, w):
        S.op("tensor", lambda e: e.matmul(out, lhsT=in_, rhs=ident, start=True, stop=True), bl(r), bl(w))

    def dma(eng, out, in_, r, w, dsem):
        S.op(eng, lambda e: e.dma_start(out=out, in_=in_), bl(r), bl(w), dma=True, dsem=dsem)

    def memset(eng, ap, val, w):
        S.op(eng, lambda e: e.memset(ap, val), [], bl(w))

    def tap(name, t_, bf=False):
        if not (DBG and tap_on[0]):
            return
        i = len(TAPS)
        TAPS.append(name)
        src = t_
        if bf:
            src = P32.get()
            S.op("vector", lambda e: e.tensor_copy(out=src.ap, in_=t_.ap), [t_.b], [src.b])
        ob = Buf()
        dma("scalar", dbg_d[i], src.ap, [src], [ob], "tap%d" % (i % 4))
        tap_bufs.append(ob)
        if bf:
            P32.put(src)

    def wload(src, shape, srcbuf):
        i = ws_i[0] % NS
        ws_i[0] += 1
        n = shape[0] * shape[1]
        view = ws_t[:, i, 0:n].rearrange("p (a b) -> p a b", b=shape[1])
        dma("sync", view, src, list(cast_last.values()), [ws_b[i]], "ws%d" % i)
        return view, ws_b[i]

    dma("sync", cst[:], cs_d, [], [cst_b], "cst")
    dma("sync", pv[:, 0, :], pv_d[0], [], [pv_b], "cst")
    dma("sync", pv[:, 1, :], pv_d[1], [], [pv_b], "cst")
    dma("sync", pbc[:, 0, :], pbc_d[0], [], [pbc_b], "cst")
    dma("sync", pbc[:, 1, :], pbc_d[1], [], [pbc_b], "cst")
    cp("vector", cbf[:], cst[:, 0:640], [cst_b], [cbf_b])
    for h in range(2):
        cp("vector", mNN[:, h * 128:(h + 1) * 128], cst[:, 128:256], [cst_b], [msk_b])
        cp("vector", mNN[:, 256 + h * 128:256 + (h + 1) * 128], cst[:, 256:384], [cst_b], [msk_b])
        memset("vector", mQ[:, h * 192:h * 192 + 64], 1.0, [msk_b])
        cp("vector", mQ[:, h * 192 + 64:(h + 1) * 192], cst[:, 256:384], [cst_b], [msk_b])
    for q in range(4):
        cp("vector", mBR[:, q * 128:(q + 1) * 128], cst[:, 384:512], [cst_b], [msk_b])
    memset("vector", onesm[:], 1.0 / 1024.0, [msk_b])
    ts1(blkmean[:], cst[:, 512:640], 1.0 / 64.0, ALU.mult, [cst_b], [msk_b])
    memset("vector", zeros16[:], 0.0, [msk_b])
    memset("vector", zprev[:], 0.0, [b for l in zprev_b for b in l])
    memset("vector", Sbf[:], 0.0, [b for l in Sbf_b for b in l])
    memset("vector", hprev[:], 0.0, [b for l in hprev_b for b in l])
    memset("vector", halo[:], 0.0, [b for l in halo_b for b in l])
    CS32 = [P32.get() for _ in range(8)]
    CS16 = [P16.get() for _ in range(8)]
    cast_i = [0]
    for l in range(2):
        for c0 in range(0, NPM, 512):
            n = min(512, NPM - c0)
            i = cast_i[0]
            cast_i[0] += 1
            st = CS32[i % 8]
            dma("sync", st.ap[:, 0:n], pm_d[l][:, c0:c0 + n], [], [st], "cl%d" % (i % 8))
            cp("vector", pm[:, l, c0:c0 + n], st.ap[:, 0:n], [st], [pm_b])
        for g in range(4):
            wsl = pm[:, l, PM["ws"] + g * 128:PM["ws"] + (g + 1) * 128]
            tt("vector", wsl, wsl, cbf[:, 384:512], ALU.mult, [pm_b, cbf_b], [pm_b])
        o = PV
        ts(pv[:, l, o["omu"]:o["omu"] + 14], pv[:, l, o["mu"]:o["mu"] + 14], -1.0, 1.0, ALU.mult, ALU.add, [pv_b], [pv_b])
        ts(pv[:, l, o["oka"]:o["oka"] + 4], pv[:, l, o["k_a"]:o["k_a"] + 4], -1.0, 1.0, ALU.mult, ALU.add, [pv_b], [pv_b])
        act(small[:, 0:4], pv[:, l, o["lam"]:o["lam"] + 4], AF.Exp, [pv_b], [small_b], scale=-1.0)
        act(small[:, 4:8], small[:, 0:4], AF.Ln, [small_b], [small_b], bias=1.0)
        ts1(pv[:, l, o["clam"]:o["clam"] + 4], small[:, 4:8], -8.0, ALU.mult, [small_b], [pv_b])
        ts1(pv[:, l, o["clam2"]:o["clam2"] + 4], small[:, 4:8], -16.0, ALU.mult, [small_b], [pv_b])

    sb = {}

    cast_last = {}

    def cast(name, dst, src, rows):
        ncol = src.shape[1]
        for r0 in range(0, rows, 128):
            for c0 in range(0, ncol, 512):
                n = min(512, ncol - c0)
                i = cast_i[0]
                cast_i[0] += 1
                a, b16 = CS32[i % 8], CS16[i % 8]
                dma("sync", a.ap[:, 0:n], src[r0:r0 + 128, c0:c0 + n], [], [a], "cl%d" % (i % 8))
                cp("vector" if i % 2 == 0 else "gpsimd", b16.ap[:, 0:n], a.ap[:, 0:n], [a], [b16])
                ob = Buf()
                dma("scalar", dst[r0:r0 + 128, c0:c0 + n], b16.ap[:, 0:n], [b16], [ob], "cs%d" % (i % 8))
                cast_last[i % 8] = ob
        return None

    for l in range(2):
        for f in range(2):
            sb["w1", l, f] = cast("w1%d%d" % (l, f), w1_s[l, f], w1_d[l, f], D)
            sb["w3", l, f] = cast("w3%d%d" % (l, f), w3_s[l, f], w3_d[l, f], D)
            sb["w2", l, f] = cast("w2%d%d" % (l, f), w2_s[l, f], w2_d[l, f], DFF)
            if f == 0:
                sb["win", l] = cast("win%d" % l, win_s[l], win_d[l], D)
                for b in range(3):
                    sb["pb", l, b] = cast("pb%d%d" % (l, b), pb_s[l, b], pb_d[l, b], 512)
                sb["wo", l] = cast("wo%d" % l, wo_s[l], wo_d[l], D)

    P32.put(*CS32)
    P16.put(*CS16)

    def pcol(l, name, j):
        return pv[:, l, PV[name] + j:PV[name] + j + 1]

    def layer_norm(l, idx, x32, xbf, contrib, coef):
        pm_, pq_ = psget(True), psget(True)
        for c in range(KC):
            pt = contrib(c)
            if pt is not None:
                stt(x32[c].ap, pt.ap, coef, x32[c].ap, ALU.mult, ALU.add, [pt, x32[c]], [x32[c]])
            yb, yq = P16.get(), P16.get()
            cp("vector", yb.ap, x32[c].ap, [x32[c]], [yb])
            tt("gpsimd", yq.ap, x32[c].ap, x32[c].ap, ALU.mult, [x32[c]], [yq])
            mm(pm_.ap, onesm[:], yb.ap, c == 0, c == KC - 1, [msk_b, yb], [pm_])
            mm(pq_.ap, onesm[:], yq.ap, c == 0, c == KC - 1, [msk_b, yq], [pq_])
            P16.put(yb, yq)
        mean, rstd = P32.get(), P32.get()
        act(mean.ap, pm_.ap, AF.Identity, [pm_], [mean])
        tt("gpsimd", rstd.ap, mean.ap, mean.ap, ALU.mult, [mean], [rstd])
        tt("vector", rstd.ap, pq_.ap, rstd.ap, ALU.subtract, [pq_, rstd], [rstd])
        act(rstd.ap, rstd.ap, AF.Ln, [rstd], [rstd], bias=LN_EPS)
        act(rstd.ap, rstd.ap, AF.Exp, [rstd], [rstd], scale=-0.5)
        for c in range(KC):
            tmp = P32.get()
            tt("gpsimd", tmp.ap, x32[c].ap, mean.ap, ALU.subtract, [x32[c], mean], [tmp])
            tt("vector", tmp.ap, tmp.ap, rstd.ap, ALU.mult, [tmp, rstd], [tmp])
            g_ = pcol(l, "ln_g", idx * 8 + c)
            b_ = pcol(l, "ln_b", idx * 8 + c)
            ts(x32[c].ap, tmp.ap, g_, b_, ALU.mult, ALU.add, [tmp, pv_b], [x32[c]])
            cp("vector", xbf[c].ap, x32[c].ap, [x32[c]], [xbf[c]])
            P32.put(tmp)
        P32.put(mean, rstd)
        psrel(pm_, pq_)

    def ffn(l, f, x32, xbf):
        g = [P16.get() for _ in range(MC)]
        for gi in range(6):
            c0 = gi * 512
            n = min(512, DFF - c0)
            w1v, w1b = wload(w1_s[l, f].rearrange("(k p) n -> p k n", p=128)[:, :, c0:c0 + n], [KC, n], sb["w1", l, f])
            w3v, w3b = wload(w3_s[l, f].rearrange("(k p) n -> p k n", p=128)[:, :, c0:c0 + n], [KC, n], sb["w3", l, f])
            for pr in range(0, n // 128, 2):
                mis = [pr, pr + 1]
                pa = {mi: psget() for mi in mis}
                pb_ = {mi: psget() for mi in mis}
                for k in range(KC):
                    for mi in mis:
                        mm(pa[mi].ap, w1v[:, k, mi * 128:(mi + 1) * 128], xbf[k].ap, k == 0, k == KC - 1, [w1b, xbf[k]], [pa[mi]])
                        mm(pb_[mi].ap, w3v[:, k, mi * 128:(mi + 1) * 128], xbf[k].ap, k == 0, k == KC - 1, [w3b, xbf[k]], [pb_[mi]])
                for mi in mis:
                    m = gi * 4 + mi
                    s = P32.get()
                    act(s.ap, pa[mi].ap, AF.Silu, [pa[mi]], [s])
                    tt("vector", g[m].ap, s.ap, pb_[mi].ap, ALU.mult, [s, pb_[mi]], [g[m]])
                    P32.put(s)
        accs = {}
        for half in range(2):
            pacc = [psget(True) for _ in range(4)]
            for mg in range(3):
                m0 = mg * 8
                nm = min(8, MC - m0)
                wv, wb = wload(w2_s[l, f].rearrange("(m p) n -> p m n", p=128)[:, m0:m0 + nm, half * 512:(half + 1) * 512], [nm, 512], sb["w2", l, f])
                order = [(mi, j) for mi in range(nm) for j in range(4)]
                if half == 1 and mg == 2:
                    order = [(mi, j) for j in range(4) for mi in range(nm)]
                for mi, j in order:
                    m = m0 + mi
                    mm(pacc[j].ap, wv[:, mi, j * 128:(j + 1) * 128], g[m].ap, m == 0, m == MC - 1, [wb, g[m]], [pacc[j]])
            for j in range(4):
                accs[half * 4 + j] = pacc[j]
            if half == 0:
                for j in range(4):
                    stt(x32[j].ap, pacc[j].ap, C_FFN, x32[j].ap, ALU.mult, ALU.add, [pacc[j], x32[j]], [x32[j]])
                    accs[j] = None
                psrel(*pacc)
        P16.put(*g)
        layer_norm(l, 0 if f == 0 else 2, x32, xbf, lambda c: accs[c], C_FFN)
        psrel(*pacc)

    def mixer(l, x32, xbf, vfirst):
        o = PV
        winv = win_s[l].rearrange("(k p) n -> p k n", p=128)

        def inproj_fm(wv, wb, mi):
            p = psget()
            for k in range(KC):
                mm(p.ap, wv[:, k, mi * 128:(mi + 1) * 128], xbf[k].ap, k == 0, k == KC - 1, [wb, xbf[k]], [p])
            return p

        zr = []
        for gi in range(4):
            c0 = gi * 512
            n = 512 if gi < 3 else 256
            wv, wb = wload(winv[:, :, c0:c0 + n], [KC, n], sb["win", l])
            nmi = n // 128
            pgs = [psget() for _ in range(nmi)]
            for k in range(KC):
                for mi in range(nmi):
                    mm(pgs[mi].ap, wv[:, k, mi * 128:(mi + 1) * 128], xbf[k].ap, k == 0, k == KC - 1, [wb, xbf[k]], [pgs[mi]])
            for mi in range(nmi):
                m = gi * 4 + mi
                p = pgs[mi]
                dd = P32.get()
                ts1(dd.ap, p.ap, pcol(l, "omu", m), ALU.mult, [p, pv_b], [dd])
                stt(dd.ap[:, 1:T], p.ap[:, 0:T - 1], pcol(l, "mu", m), dd.ap[:, 1:T], ALU.mult, ALU.add, [p, dd, pv_b], [dd])
                stt(dd.ap[:, 0:1], zprev[:, l, m:m + 1], pcol(l, "mu", m), dd.ap[:, 0:1], ALU.mult, ALU.add, [zprev_b[l][m], dd, pv_b], [dd])
                cp("vector", zprev[:, l, m:m + 1], p.ap[:, T - 1:T], [p], [zprev_b[l][m]])
                zr.append(dd)
                tap("zr_%d:%d" % (l, m), dd)
        wain, sgin = P16.get(), P16.get()
        act(wain.ap[0:64, :], zr[12].ap[0:64, :], AF.Tanh, [zr[12]], [wain])
        act(wain.ap[64:128, :], zr[12].ap[64:128, :], AF.Identity, [zr[12]], [wain])
        act(sgin.ap, zr[13].ap, AF.Sigmoid, [zr[13]], [sgin])
        P32.put(zr[12], zr[13])
        pmv = lambda name, c0, n: pm[:, l, PM[name] + c0:PM[name] + c0 + n]
        lo_bf = None
        if l == 1:
            vb = []
            plo = psget(True)
            for hp in range(4):
                t_ = P16.get()
                cp("vector", t_.ap, zr[8 + hp].ap, [zr[8 + hp]], [t_])
                vb.append(t_)
                mm(plo.ap[0:32, :], pmv("v1", hp * 32, 32), t_.ap, hp == 0, hp == 3, [pm_b, t_], [plo])
            lo_bf = P16.get()
            cp("vector", lo_bf.ap[0:32, :], plo.ap[0:32, :], [plo], [lo_bf])
            psrel(plo)
            P16.put(*vb)
        o_rwkv = []
        for hp in range(4):
            r32, k32, v32 = zr[hp], zr[4 + hp], zr[8 + hp]
            if l == 0:
                vfirst.append(v32)
            else:
                p = psget()
                mm(p.ap, pm[0:32, l, PM["v2"] + hp * 128:PM["v2"] + (hp + 1) * 128], lo_bf.ap[0:32, :], True, True, [pm_b, lo_bf], [p])
                sg_ = P32.get()
                act(sg_.ap, p.ap, AF.Sigmoid, [p, pv_b], [sg_], bias=pcol(l, "v0", hp))
                dv = P32.get()
                tt("gpsimd", dv.ap, vfirst[hp].ap, v32.ap, ALU.subtract, [vfirst[hp], v32], [dv])
                tt("vector", dv.ap, dv.ap, sg_.ap, ALU.mult, [dv, sg_], [dv])
                tt("gpsimd", v32.ap, v32.ap, dv.ap, ALU.add, [v32, dv], [v32])
                P32.put(sg_, dv, vfirst[hp])
            p = psget()
            mm(p.ap, pm[0:64, l, PM["wa"] + hp * 128:PM["wa"] + (hp + 1) * 128], wain.ap[0:64, :], True, True, [pm_b, wain], [p])
            lw = P32.get()
            act(lw.ap, p.ap, AF.Sigmoid, [p, pv_b], [lw], bias=pcol(l, "w0", hp))
            p = psget()
            mm(p.ap, pm[64:128, l, PM["wa"] + hp * 128:PM["wa"] + (hp + 1) * 128], wain.ap[64:128, :], True, True, [pm_b, wain], [p])
            a32 = P32.get()
            act(a32.ap, p.ap, AF.Sigmoid, [p, pv_b], [a32], bias=pcol(l, "a0", hp))
            p = psget()
            mm(p.ap, pmv("g2", hp * 128, 128), sgin.ap, True, True, [pm_b, sgin], [p])
            g32 = P32.get()
            act(g32.ap, p.ap, AF.Identity, [p], [g32])
            kk = P32.get()
            ts1(kk.ap, k32.ap, pcol(l, "k_k", hp), ALU.mult, [k32, pv_b], [kk])
            kq = P16.get()
            tt("gpsimd", kq.ap, kk.ap, kk.ap, ALU.mult, [kk], [kq])
            p = psget()
            mm(p.ap, blkb, kq.ap, True, True, [cbf_b, kq], [p])
            P16.put(kq)
            nr = P32.get()
            act(nr.ap, p.ap, AF.Ln, [p], [nr], bias=1e-24)
            act(nr.ap, nr.ap, AF.Exp, [nr], [nr], scale=-0.5)
            tt("gpsimd", kk.ap, kk.ap, nr.ap, ALU.mult, [kk, nr], [kk])
            P32.put(nr)
            kf = P32.get()
            ts(kf.ap, a32.ap, pcol(l, "k_a", hp), pcol(l, "oka", hp), ALU.mult, ALU.add, [a32, pv_b], [kf])
            tt("vector", kf.ap, kf.ap, k32.ap, ALU.mult, [kf, k32], [kf])
            P32.put(k32)
            rk = P16.get()
            stt(rk.ap, r32.ap, pcol(l, "r_k", hp), kf.ap, ALU.mult, ALU.mult, [r32, kf, pv_b], [rk])
            p = psget()
            mm(p.ap, blkb, rk.ap, True, True, [cbf_b, rk], [p])
            P16.put(rk)
            bon = P32.get()
            tt("vector", bon.ap, p.ap, v32.ap, ALU.mult, [p, v32], [bon])
            L, Lm = P32.get(), P32.get()
            S.op("vector", lambda e, L=L, lw=lw: e.tensor_tensor_scan(out=L.ap, data0=cst[:, 640:1152], data1=lw.ap, initial=0.0, op0=ALU.mult, op1=ALU.add), [cst_b, lw.b], [L.b])
            tt("gpsimd", Lm.ap, L.ap, lw.ap, ALU.subtract, [L, lw], [Lm])
            tap("lw_%d:%d" % (l, hp), lw)
            tap("a_%d:%d" % (l, hp), a32)
            P32.put(lw)
            eL, enL, eC = P32.get(), P32.get(), P32.get()
            act(eL.ap, L.ap, AF.Exp, [L], [eL], scale=NEG_E)
            act(Lm.ap, Lm.ap, AF.Exp, [Lm], [Lm], scale=NEG_E)
            act(enL.ap, L.ap, AF.Exp, [L], [enL], scale=-NEG_E)
            L3 = L.ap.rearrange("p (c t) -> p c t", t=128)
            S.op("vector", lambda e, L3=L3, eC=eC: e.tensor_tensor(out=eC.ap.rearrange("p (c t) -> p c t", t=128), in0=L3[:, :, 127:128].broadcast_to([128, 4, 128]), in1=L3, op=ALU.subtract), [L.b], [eC.b])
            act(eC.ap, eC.ap, AF.Exp, [eC], [eC], scale=NEG_E)
            P32.put(L)
            bb = P32.get()
            tt("gpsimd", bb.ap, kk.ap, a32.ap, ALU.mult, [kk, a32], [bb])
            P32.put(a32)
            Ap, Rp, Bm, Km, BC, KCc, vbf = [P16.get() for _ in range(7)]
            stt(Ap.ap, kk.ap, -1.0, Lm.ap, ALU.mult, ALU.mult, [kk, Lm], [Ap])
            tt("vector", Rp.ap, r32.ap, eL.ap, ALU.mult, [r32, eL], [Rp])
            tt("gpsimd", Bm.ap, bb.ap, enL.ap, ALU.mult, [bb, enL], [Bm])
            tt("gpsimd", Km.ap, kf.ap, enL.ap, ALU.mult, [kf, enL], [Km])
            tt("gpsimd", BC.ap, bb.ap, eC.ap, ALU.mult, [bb, eC], [BC])
            tt("gpsimd", KCc.ap, kf.ap, eC.ap, ALU.mult, [kf, eC], [KCc])
            cp("vector", vbf.ap, v32.ap, [v32], [vbf])
            tap("kk_%d:%d" % (l, hp), kk)
            tap("kf_%d:%d" % (l, hp), kf)
            tap("v_%d:%d" % (l, hp), v32)
            tap("g_%d:%d" % (l, hp), g32)
            P32.put(kk, Lm, enL, eC, bb, kf, r32)
            if l == 1:
                P32.put(v32)
            pO = psget(True)
            units = [dict(NN=[P16.get(), P16.get()], Q=P16.get(), BR=P16.get(), TT=P16.get()) for _ in range(2)]
            Sv = Sbf[:, l, hp, :]
            Sb_ = Sbf_b[l][hp]
            for pair in range(2):
                for ui in range(2):
                    u = units[ui]
                    c = pair * 2 + ui
                    u["c"] = c
                    cs = slice(c * 128, (c + 1) * 128)
                    u["cs"] = cs
                    p0, p1, p2, p3 = psget(), psget(), psget(), psget()
                    for h in range(2):
                        hs = slice(64 * h, 64 * h + 64)
                        rd = [Ap, Bm, Km, Rp, BC, vbf, cbf_b]
                        mm(p0.ap[:, h * 128:(h + 1) * 128], Bm.ap[hs, cs], Ap.ap[hs, cs], True, True, rd, [p0])
                        mm(p0.ap[:, 256 + h * 128:256 + (h + 1) * 128], Ap.ap[hs, cs], Bm.ap[hs, cs], True, True, rd, [p0])
                        mm(p1.ap[:, h * 192:h * 192 + 64], Ap.ap[hs, cs], cbf[hs, 64 * h:64 * h + 64], True, True, rd, [p1])
                        mm(p1.ap[:, h * 192 + 64:(h + 1) * 192], Ap.ap[hs, cs], Km.ap[hs, cs], True, True, rd, [p1])
                        mm(p2.ap[:, h * 128:(h + 1) * 128], Bm.ap[hs, cs], Rp.ap[hs, cs], True, True, rd, [p2])
                        mm(p2.ap[:, 256 + h * 128:256 + (h + 1) * 128], Km.ap[hs, cs], Rp.ap[hs, cs], True, True, rd, [p2])
                        mm(p3.ap[:, h * 64:(h + 1) * 64], BC.ap[hs, cs], cbf[hs, 64 * h:64 * h + 64], True, True, rd, [p3])
                        mm(p3.ap[:, 128 + h * 64:128 + (h + 1) * 64], vbf.ap[hs, cs], cbf[hs, 64 * h:64 * h + 64], True, True, rd, [p3])
                    u["cur"] = 0
                    NN, Q, BR, TTt = u["NN"], u["Q"], u["BR"], u["TT"]
                    tt("vector", NN[0].ap, p0.ap, mNN[:], ALU.mult, [p0, msk_b], [NN[0]])
                    tt("vector", Q.ap[:, 0:384], p1.ap[:, 0:384], mQ[:], ALU.mult, [p1, msk_b], [Q])
                    tt("vector", BR.ap, p2.ap, mBR[:], ALU.mult, [p2, msk_b], [BR])
                    act(TTt.ap[:, 0:256], p3.ap[:, 0:256], AF.Identity, [p3], [TTt])
                for lev in range(7):
                    pend = []
                    for u in units:
                        NN, Q = u["NN"], u["Q"]
                        N_ = NN[u["cur"]]
                        pq = psget()
                        for h in range(2):
                            mm(pq.ap[:, h * 192:(h + 1) * 192], N_.ap[:, h * 128:(h + 1) * 128], Q.ap[:, h * 192:(h + 1) * 192], True, True, [N_, Q], [pq])
                        pn = None
                        if lev < 6:
                            pn = psget()
                            for h in range(2):
                                mm(pn.ap[:, h * 128:(h + 1) * 128], N_.ap[:, 256 + h * 128:256 + (h + 1) * 128], N_.ap[:, h * 128:(h + 1) * 128], True, True, [N_], [pn])
                                if lev < 5:
                                    mm(pn.ap[:, 256 + h * 128:256 + (h + 1) * 128], N_.ap[:, h * 128:(h + 1) * 128], N_.ap[:, 256 + h * 128:256 + (h + 1) * 128], True, True, [N_], [pn])
                        pend.append((pq, pn))
                    for u, (pq, pn) in zip(units, pend):
                        NN, Q = u["NN"], u["Q"]
                        if lev < 6:
                            wd = 512 if lev < 5 else 256
                            act(NN[1 - u["cur"]].ap[:, 0:wd], pn.ap[:, 0:wd], AF.Identity, [pn], [NN[1 - u["cur"]]])
                            u["cur"] = 1 - u["cur"]
                        tt("vector", Q.ap[:, 0:384], pq.ap[:, 0:384], Q.ap[:, 0:384], ALU.add, [pq, Q], [Q])
                for u in units:
                    NN, Q, BR, TTt, c, cs = u["NN"], u["Q"], u["BR"], u["TT"], u["c"], u["cs"]
                    XH = NN[u["cur"]]
                    YY = NN[1 - u["cur"]]
                    u["XH"], u["YY"] = XH, YY
                    pd, py = psget(), psget()
                    for h in range(2):
                        mm(pd.ap[64 * h:64 * h + 64, 0:128], Q.ap[:, h * 192:h * 192 + 64], BR.ap[:, h * 128:(h + 1) * 128], True, True, [Q, BR], [pd], tp=(0, 64 * h))
                        mm(pd.ap[:, 128 + h * 128:256 + h * 128], Q.ap[:, h * 192 + 64:(h + 1) * 192], BR.ap[:, h * 128:(h + 1) * 128], True, True, [Q, BR], [pd])
                    mm(py.ap[:, 0:128], zeros16[:], identb, True, False, [msk_b, cbf_b], [py])
                    for h in range(2):
                        mm(py.ap[64 * h:64 * h + 64, 64 * h:64 * h + 64], Q.ap[:, h * 192:h * 192 + 64], TTt.ap[:, h * 64:(h + 1) * 64], False, False, [Q, TTt], [py], tp=(0, 64 * h))
                    mm(py.ap[:, 0:128], zeros16[:], identb, False, True, [msk_b, cbf_b], [py])
                    for h in range(2):
                        hs = slice(64 * h, 64 * h + 64)
                        mm(py.ap[:, 128 + h * 64:128 + (h + 1) * 64], Q.ap[:, h * 192 + 64:(h + 1) * 192], TTt.ap[:, h * 64:(h + 1) * 64], True, False, [Q, TTt], [py])
                        mm(py.ap[:, 128 + h * 64:128 + (h + 1) * 64], KCc.ap[hs, cs], cbf[hs, 64 * h:64 * h + 64], False, True, [KCc, cbf_b], [py])
                    tt("vector", XH.ap[:, 0:128], pd.ap[:, 0:128], Rp.ap[:, cs], ALU.add, [pd, Rp], [XH])
                    tt("vector", XH.ap[:, 128:384], pd.ap[:, 128:384], BR.ap[:, 256:512], ALU.add, [pd, BR], [XH])
                    stt(YY.ap[:, 0:128], ident32, eL.ap[:, c * 128 + 127:c * 128 + 128], py.ap[:, 0:128], ALU.mult, ALU.add, [cst_b, eL, py], [YY])
                    act(XH.ap[:, 384:512], py.ap[:, 128:256], AF.Identity, [py], [XH])
                for u in units:
                    XH, YY, TTt, cs = u["XH"], u["YY"], u["TT"], u["cs"]
                    for h in range(2):
                        mm(pO.ap[64 * h:64 * h + 64, cs], TTt.ap[:, 128 + h * 64:128 + (h + 1) * 64], XH.ap[:, 128 + h * 128:256 + h * 128], True, False, [TTt, XH], [pO], tp=(0, 64 * h))
                    mm(pO.ap[:, cs], Sv, XH.ap[:, 0:128], False, True, [Sb_, XH], [pO])
                    pS = psget()
                    mm(pS.ap[:, 0:128], YY.ap[:, 0:128], Sv, True, False, [YY, Sb_], [pS])
                    mm(pS.ap[:, 0:128], XH.ap[:, 384:512], TTt.ap[:, 128:256], False, True, [XH, TTt], [pS])
                    tt("vector", Sv, pS.ap[:, 0:128], blkb, ALU.mult, [pS, cbf_b], [Sb_])
            for u in units:
                P16.put(u["NN"][0], u["NN"][1], u["Q"], u["BR"], u["TT"])
            P16.put(Ap, Rp, Bm, Km, BC, KCc, vbf)
            P32.put(eL)
            o32 = P32.get()
            act(o32.ap, pO.ap, AF.Identity, [pO], [o32])
            tap("wkv_%d:%d" % (l, hp), o32)
            psrel(pO)
            ob = P16.get()
            cp("vector", ob.ap, o32.ap, [o32], [ob])
            p = psget()
            mm(p.ap, blkmean[:], ob.ap, True, True, [msk_b, ob], [p])
            tt("vector", o32.ap, o32.ap, p.ap, ALU.subtract, [o32, p], [o32])
            tt("gpsimd", ob.ap, o32.ap, o32.ap, ALU.mult, [o32], [ob])
            p = psget()
            mm(p.ap, blkmean[:], ob.ap, True, True, [msk_b, ob], [p])
            P16.put(ob)
            sd = P32.get()
            act(sd.ap, p.ap, AF.Ln, [p], [sd], bias=GN_EPS)
            act(sd.ap, sd.ap, AF.Exp, [sd], [sd], scale=-0.5)
            tt("vector", o32.ap, o32.ap, sd.ap, ALU.mult, [o32, sd], [o32])
            P32.put(sd)
            ts(o32.ap, o32.ap, pcol(l, "gn_g", hp), pcol(l, "gn_b", hp), ALU.mult, ALU.add, [o32, pv_b], [o32])
            tt("gpsimd", o32.ap, o32.ap, bon.ap, ALU.add, [o32, bon], [o32])
            of = P16.get()
            tt("vector", of.ap, o32.ap, g32.ap, ALU.mult, [o32, g32], [of])
            P32.put(o32, bon, g32)
            o_rwkv.append(of)
            tap("o_rwkv_%d:%d" % (l, hp), of, True)
        P16.put(wain, sgin)
        if l == 1:
            P16.put(lo_bf)
        wv, wb = wload(winv[:, :, 1792:2304], [KC, 512], sb["win", l])
        u32 = []
        for g in range(4):
            p = inproj_fm(wv, wb, g)
            u = P32.get()
            act(u.ap, p.ap, AF.Gelu_apprx_tanh, [p], [u])
            u32.append(u)
        wv, wb = wload(winv[:, :, 2304:2816], [KC, 512], sb["win", l])
        vtm = []
        sm_bs = [Buf() for _ in range(4)]
        for tc in range(4):
            small_b = sm_bs[tc]
            p = psget()
            for k in range(KC):
                mm(p.ap, xbf[k].ap[:, tc * 128:(tc + 1) * 128], wv[:, k, :], k == 0, k == KC - 1, [wb, xbf[k]], [p])
            gv = P32.get()
            act(gv.ap, p.ap, AF.Gelu_apprx_tanh, [p], [gv])
            sm = small[:, 8 + tc:9 + tc]
            S.op("vector", lambda e, gv=gv, sm=sm: e.reduce_sum(out=sm, in_=gv.ap, axis=AX.X), [gv.b], [small_b])
            ts1(sm, sm, -1.0 / 512.0, ALU.mult, [small_b], [small_b])
            ts1(gv.ap, gv.ap, sm, ALU.add, [gv, small_b], [gv])
            sq = P32.get()
            tt("gpsimd", sq.ap, gv.ap, gv.ap, ALU.mult, [gv], [sq])
            sv_ = small[:, 12 + tc:13 + tc]
            S.op("vector", lambda e, sq=sq, sv_=sv_: e.reduce_sum(out=sv_, in_=sq.ap, axis=AX.X), [sq.b], [small_b])
            P32.put(sq)
            act(sv_, sv_, AF.Sqrt, [small_b], [small_b], bias=1e-5, scale=1.0 / 512.0)
            S.op("vector", lambda e, sv_=sv_: e.reciprocal(out=sv_, in_=sv_), [small_b], [small_b])
            ts1(gv.ap, gv.ap, sv_, ALU.mult, [gv, small_b], [gv])
            tt("vector", gv.ap, gv.ap, pbc[:, l, 0:512], ALU.mult, [gv, pbc_b], [gv])
            vt = P16.get()
            tt("vector", vt.ap, gv.ap, pbc[:, l, 512:1024], ALU.add, [gv, pbc_b], [vt])
            P32.put(gv)
            vtm.append(vt)
        o_gmlp = []
        for g in range(4):
            p = psget()
            for tc in range(4):
                mm(p.ap[:, tc * 128:(tc + 1) * 128], vtm[tc].ap[:, g * 128:(g + 1) * 128], pm[:, l, PM["ws"] + g * 128:PM["ws"] + (g + 1) * 128], True, True, [vtm[tc], pm_b], [p])
            tmp = P32.get()
            S.op("vector", lambda e, tmp=tmp, p=p, g=g: e.tensor_tensor(out=tmp.ap.rearrange("p (c t) -> p c t", t=128), in0=p.ap.rearrange("p (c t) -> p c t", t=128), in1=pbc[:, l, 1024 + g * 128:1024 + (g + 1) * 128].rearrange("p (c t) -> p c t", c=1).broadcast_to([128, 4, 128]), op=ALU.add), [p.b, pbc_b], [tmp.b])
            og = P16.get()
            tt("vector", og.ap, tmp.ap, u32[g].ap, ALU.mult, [tmp, u32[g]], [og])
            P32.put(tmp, u32[g])
            o_gmlp.append(og)
            tap("o_gmlp_%d:%d" % (l, g), og, True)
        P16.put(*vtm)
        wv, wb = wload(winv[:, :, 2816:3328], [KC, 512], sb["win", l])
        for c in range(4):
            p = inproj_fm(wv, wb, c)
            cp("gpsimd", zxb[:, c, 0:3], halo[:, l, c, :], [halo_b[l][c]], [zxb_b[c]])
            act(zxb[:, c, 3:515], p.ap, AF.Identity, [p], [zxb_b[c]])
            cp("gpsimd", halo[:, l, c, :], zxb[:, c, 512:515], [zxb_b[c]], [halo_b[l][c]])
        wv, wb = wload(winv[:, :, 3328:3840], [KC, 512], sb["win", l])
        o_lru = []
        for c in range(4):
            p = inproj_fm(wv, wb, c)
            y32 = P32.get()
            act(y32.ap, p.ap, AF.Gelu_apprx_tanh, [p], [y32])
            xc = P32.get()
            cw = lambda j: pv[:, l, PV["cw"] + j * 4 + c:PV["cw"] + j * 4 + c + 1]
            ts(xc.ap, zxb[:, c, 3:515], cw(3), pcol(l, "cb", c), ALU.mult, ALU.add, [zxb_b[c], pv_b], [xc])
            for j in range(3):
                stt(xc.ap, zxb[:, c, j:j + 512], cw(j), xc.ap, ALU.mult, ALU.add, [zxb_b[c], pv_b, xc], [xc])
            xcb = P16.get()
            cp("vector", xcb.ap, xc.ap, [xc], [xcb])
            pr, pi = psget(), psget()
            mm(pr.ap, pm[:, l, PM["lwa"] + c * 128:PM["lwa"] + (c + 1) * 128], xcb.ap, True, True, [pm_b, xcb], [pr])
            mm(pi.ap, pm[:, l, PM["lwx"] + c * 128:PM["lwx"] + (c + 1) * 128], xcb.ap, True, True, [pm_b, xcb], [pi])
            P16.put(xcb)
            rg, ig, aa = P32.get(), P32.get(), P32.get()
            act(rg.ap, pr.ap, AF.Sigmoid, [pr, pv_b], [rg], bias=pcol(l, "ba", c))
            act(ig.ap, pi.ap, AF.Sigmoid, [pi, pv_b], [ig], bias=pcol(l, "bx", c))
            act(aa.ap, rg.ap, AF.Exp, [rg, pv_b], [aa], scale=pcol(l, "clam", c))
            act(rg.ap, rg.ap, AF.Exp, [rg, pv_b], [rg], scale=pcol(l, "clam2", c))
            act(rg.ap, rg.ap, AF.Ln, [rg], [rg], bias=1.0, scale=-1.0)
            act(rg.ap, rg.ap, AF.Exp, [rg], [rg], scale=0.5)
            tt("gpsimd", ig.ap, ig.ap, xc.ap, ALU.mult, [ig, xc], [ig])
            tt("vector", ig.ap, ig.ap, rg.ap, ALU.mult, [ig, rg], [ig])
            hh = xc
            S.op("vector", lambda e, hh=hh, aa=aa, ig=ig, c=c: e.tensor_tensor_scan(out=hh.ap, data0=aa.ap, data1=ig.ap, initial=hprev[:, l, c:c + 1], op0=ALU.mult, op1=ALU.add), [aa.b, ig.b, hprev_b[l][c]], [hh.b])
            cp("gpsimd", hprev[:, l, c:c + 1], hh.ap[:, T - 1:T], [hh], [hprev_b[l][c]])
            ol = P16.get()
            tt("vector", ol.ap, hh.ap, y32.ap, ALU.mult, [hh, y32], [ol])
            P32.put(rg, ig, aa, xc, y32)
            o_lru.append(ol)
            tap("o_lru_%d:%d" % (l, c), ol, True)
        branches = [o_rwkv, o_gmlp, o_lru]
        merged = [P32.get() for _ in range(KC)]
        for b in range(3):
            pbv, pbb = wload(pb_s[l, b].rearrange("(k p) n -> p k n", p=128), [4, D], sb["pb", l, b])
            for half in range(2):
                c0 = OFF_GATE + b * D + half * 512
                wv, wb = wload(winv[:, :, c0:c0 + 512], [KC, 512], sb["win", l])
                for jj in range(4):
                    j = half * 4 + jj
                    pg = inproj_fm(wv, wb, jj)
                    pp = psget()
                    for k in range(4):
                        mm(pp.ap, pbv[:, k, j * 128:(j + 1) * 128], branches[b][k].ap, k == 0, k == 3, [pbb, branches[b][k]], [pp])
                    sg_ = P32.get()
                    act(sg_.ap, pg.ap, AF.Sigmoid, [pg, pv_b], [sg_], bias=pcol(l, "gate_b", b * 8 + j))
                    if b == 0:
                        tt("vector", merged[j].ap, sg_.ap, pp.ap, ALU.mult, [sg_, pp], [merged[j]])
                    else:
                        tt("vector", sg_.ap, sg_.ap, pp.ap, ALU.mult, [sg_, pp], [sg_])
                        tt("gpsimd", merged[j].ap, merged[j].ap, sg_.ap, ALU.add, [merged[j], sg_], [merged[j]])
                    P32.put(sg_)
            P16.put(*branches[b])
        for j in range(KC):
            tap("merged_%d:%d" % (l, j), merged[j])
        mbf = []
        for j in range(KC):
            t_ = P16.get()
            cp("vector", t_.ap, merged[j].ap, [merged[j]], [t_])
            mbf.append(t_)
        P32.put(*merged)
        wov = wo_s[l].rearrange("(k p) n -> p k n", p=128)
        held = {}

        def contrib(c):
            half, jj = divmod(c, 4)
            if half not in held:
                held[half] = wload(wov[:, :, half * 512:(half + 1) * 512], [KC, 512], sb["wo", l])
            wv, wb = held[half]
            p = psget()
            for k in range(KC):
                mm(p.ap, wv[:, k, jj * 128:(jj + 1) * 128], mbf[k].ap, k == 0, k == KC - 1, [wb, mbf[k]], [p])
            return p

        for c in range(KC):
            p = contrib(c)
            stt(x32[c].ap, p.ap, C_MIX, x32[c].ap, ALU.mult, ALU.add, [p, x32[c]], [x32[c]])
        P16.put(*mbf)
        layer_norm(l, 1, x32, xbf, lambda c: None, C_MIX)

    x32 = [P32.get() for _ in range(KC)]
    xbf = [P16.get() for _ in range(KC)]
    out_bufs = []

    def load_x(ti):
        tiles = []
        for tc in range(4):
            for hf in range(2):
                t_ = P32.get()
                dma("sync", t_.ap, x_d[ti * T + tc * 128:ti * T + (tc + 1) * 128, hf * 512:(hf + 1) * 512], [], [t_], "xin%d" % (tc * 2 + hf))
                tiles.append(t_)
        return tiles

    import os
    LIM = int(os.environ.get("KLIM", "99"))

    class StopBuild(Exception):
        pass

    def ckpt(n):
        if n > LIM:
            raise StopBuild()

    try:
      ckpt(1)
      xin = load_x(0)
      for ti in range(NT):
            hl = []
            for t_ in xin:
                hi, lo = P16.get(), P16.get()
                cp("vector", hi.ap, t_.ap, [t_], [hi])
                tt("vector", lo.ap, t_.ap, hi.ap, ALU.subtract, [t_, hi], [lo])
                hl.append((hi, lo))
            for k in range(KC):
                p = psget()
                for tc in range(4):
                    hi, lo = hl[tc * 2 + k // 4]
                    csl = slice((k % 4) * 128, (k % 4 + 1) * 128)
                    mm(p.ap[:, tc * 128:(tc + 1) * 128], hi.ap[:, csl], identb, True, False, [hi, cbf_b], [p])
                    mm(p.ap[:, tc * 128:(tc + 1) * 128], lo.ap[:, csl], identb, False, True, [lo, cbf_b], [p])
                act(x32[k].ap, p.ap, AF.Identity, [p], [x32[k]])
                cp("vector", xbf[k].ap, p.ap, [p], [xbf[k]])
            for hi, lo in hl:
                P16.put(hi, lo)
            P32.put(*xin)
            ckpt(2)
            vfirst = []
            for l in range(2):
                ckpt(3)
                tap_on[0] = (ti == 0)
                ffn(l, 0, x32, xbf)
                for k in range(KC):
                    tap("x_%d_f0:%d" % (l, k), x32[k])
                ckpt(4)
                mixer(l, x32, xbf, vfirst)
                for k in range(KC):
                    tap("x_%d_mix:%d" % (l, k), x32[k])
                ckpt(5)
                if l == 1 and ti + 1 < NT:
                    xin = load_x(ti + 1)
                ffn(l, 1, x32, xbf)
                for k in range(KC):
                    tap("x_%d_f1:%d" % (l, k), x32[k])
            hl = []
            for k in range(KC):
                hi, lo = P16.get(), P16.get()
                cp("vector", hi.ap, x32[k].ap, [x32[k]], [hi])
                tt("vector", lo.ap, x32[k].ap, hi.ap, ALU.subtract, [x32[k], hi], [lo])
                hl.append((hi, lo))
            for tc in range(4):
                for hf in range(2):
                    p = psget()
                    for kk_ in range(4):
                        hi, lo = hl[hf * 4 + kk_]
                        mm(p.ap[:, kk_ * 128:(kk_ + 1) * 128], hi.ap[:, tc * 128:(tc + 1) * 128], identb, True, False, [hi, cbf_b], [p])
                        mm(p.ap[:, kk_ * 128:(kk_ + 1) * 128], lo.ap[:, tc * 128:(tc + 1) * 128], identb, False, True, [lo, cbf_b], [p])
                    t_ = P32.get()
                    act(t_.ap, p.ap, AF.Identity, [p], [t_])
                    ob = Buf()
                    dma("sync", out_d[ti * T + tc * 128:ti * T + (tc + 1) * 128, hf * 512:(hf + 1) * 512], t_.ap, [t_], [ob], "out%d" % (tc * 2 + hf))
                    out_bufs.append(ob)
                    P32.put(t_)
            for hi, lo in hl:
                P16.put(hi, lo)

    except StopBuild:
        pass
    out_bufs = out_bufs + tap_bufs
    if LIM < 99:
        out_bufs = out_bufs + list(cast_last.values())
    cnt = S.emit(final_bufs=out_bufs)
    print("SEMS", {k: v for k, v in cnt.items() if k[0] == "eng"}, len(cnt), "nops", len(S.ops))
    es.close()
    return nc


_CACHE = {}
TAPS = []
LAST_DBG = None


def run(inputs, SEQ, n_cores, trace=False):
    pv, pm, pb = host_params(inputs)
    cs = host_consts()
    if SEQ not in _CACHE:
        _CACHE[SEQ] = build(SEQ)
    nc = _CACHE[SEQ]
    f = lambda k: np.ascontiguousarray(np.asarray(inputs[k], np.float32))
    shared = {"ffn_w1": f("ffn_w1"), "ffn_w3": f("ffn_w3"), "ffn_w2": f("ffn_w2"), "w_in": f("w_in"),
              "p_branch": f("p_branch"), "w_out": f("w_out"), "pvec": pv, "pmat": pm, "pbc": pb, "cst": cs}
    x = np.asarray(inputs["x"], np.float32)
    in_maps = [dict(shared, x=np.ascontiguousarray(x[c, :SEQ])) for c in range(n_cores)]
    res = run_bass_kernel_spmd(nc, in_maps, core_ids=list(range(n_cores)), **({"trace": True} if trace else {}))
    if trace:
        print("EXEC_NS", res.exec_time_ns)
    global LAST_DBG
    LAST_DBG = np.asarray(res.results[0]["dbg"]) if "dbg" in res.results[0] else None
    return np.stack([np.asarray(r["out"]) for r in res.results], axis=0)


def kernel(**inputs):
    return run(inputs, 8192, 8).astype(np.float32)
```

```python
from contextlib import ExitStack
import numpy as np
import concourse.bass as bass
import concourse.mybir as mybir
from concourse.bass_utils import run_bass_kernel_spmd

F32 = mybir.dt.float32
BF16 = mybir.dt.bfloat16
AF = mybir.ActivationFunctionType
ALU = mybir.AluOpType
AX = mybir.AxisListType

ENGS = ("sync", "scalar", "gpsimd", "vector", "tensor")


class Buf:
    __slots__ = ("name", "last_w", "readers")

    def __init__(self, name=""):
        self.name = name
        self.last_w = None
        self.readers = []


class Op:
    __slots__ = ("eng", "fn", "deps", "is_dma", "sig", "idx", "eidx", "dsem", "waits", "need_sig")


class Sched:
    def __init__(self, nc):
        self.nc = nc
        self.ops = []
        self.eng_ops = {e: [] for e in ENGS}

    def op(self, eng, fn, reads=(), writes=(), dma=False, dsem=None):
        o = Op()
        o.eng = eng
        o.fn = fn
        o.is_dma = dma
        o.dsem = dsem
        o.deps = []
        o.sig = None
        o.need_sig = dma
        o.waits = []
        o.idx = len(self.ops)
        o.eidx = len(self.eng_ops[eng])
        deps = {}
        for r in reads:
            if r.last_w is not None:
                deps[r.last_w.idx] = (r.last_w, "raw")
        for w in writes:
            if w.last_w is not None and w.last_w.idx not in deps:
                deps[w.last_w.idx] = (w.last_w, "waw")
            for rd in w.readers:
                if rd.idx not in deps:
                    deps[rd.idx] = (rd, "war")
        for r in reads:
            r.readers.append(o)
        for w in writes:
            w.last_w = o
            w.readers = []
        for p, kind in deps.values():
            if p is o:
                continue
            if p.eng == eng and not p.is_dma and not dma:
                if eng == "tensor":
                    continue
            o.deps.append(p)
            p.need_sig = True
        self.ops.append(o)
        self.eng_ops[eng].append(o)
        return o

    def emit(self, final_bufs=()):
        nc = self.nc
        final_ops = []
        for b in final_bufs:
            if b.last_w is not None:
                final_ops.append(b.last_w)
            final_ops.extend(b.readers)
        for p in final_ops:
            p.need_sig = True
        counter = {}
        waited = {e: {} for e in ENGS}
        for o in self.ops:
            for p in o.deps:
                key = p.sig[0]
                val = counter[key] if p.is_dma else p.sig[1]
                if waited[o.eng].get(key, 0) >= val:
                    continue
                waited[o.eng][key] = val
                o.waits.append((key, val))
            if o.need_sig:
                key = ("dma", o.dsem) if o.is_dma else ("eng", o.eng)
                counter[key] = counter.get(key, 0) + (16 if o.is_dma else 1)
                o.sig = (key, counter[key])
        fin = {}
        for p in final_ops:
            key = p.sig[0]
            val = counter[key] if p.is_dma else p.sig[1]
            fin[key] = max(fin.get(key, 0), val)
        with ExitStack() as es:
            sems = {}
            for i, key in enumerate(counter):
                sems[key] = es.enter_context(nc.semaphore("s%d" % i))
            block = es.enter_context(nc.Block())
            for e in ENGS:
                ops = self.eng_ops[e]
                is_last = e == "sync"

                def body(eng, ops=ops, is_last=is_last):
                    for o in ops:
                        for key, val in o.waits:
                            eng.wait_ge(sems[key], val)
                        inst = o.fn(eng)
                        if o.need_sig:
                            inst.then_inc(sems[o.sig[0]], 16 if o.is_dma else 1)
                    if is_last:
                        for key, val in fin.items():
                            eng.wait_ge(sems[key], val)

                getattr(block, e)(body)
        return counter


class Tile:
    __slots__ = ("ap", "b")

    def __init__(self, ap, b):
        self.ap = ap
        self.b = b


class Pool:
    def __init__(self, tiles):
        self.free = list(tiles)

    def get(self):
        return self.free.pop(0)

    def put(self, *ts):
        for t in ts:
            self.free.append(t)


D = 1024
KC = 8
DFF = 2816
MC = 22
NIN = 6912
T = 512
DEPTH = 2
ALPHA = (2 * DEPTH) ** 0.25
C_FFN = 0.5 / ALPHA
C_MIX = 1.0 / ALPHA
LN_EPS = 1e-5 / (ALPHA * ALPHA)
GN_EPS = 64e-5
NEG_E = -float(np.exp(-0.5))
OFF_GATE = 3840

PV = {}
_o = 0
for _n, _w in [("mu", 14), ("ln_g", 24), ("ln_b", 24), ("gate_b", 24), ("w0", 4), ("a0", 4), ("k_k", 4), ("k_a", 4),
               ("r_k", 4), ("gn_g", 4), ("gn_b", 4), ("v0", 4), ("cw", 16), ("cb", 4), ("ba", 4), ("bx", 4), ("lam", 4),
               ("omu", 14), ("oka", 4), ("clam", 4), ("clam2", 4)]:
    PV[_n] = _o
    _o += _w
NPV = _o
PM = {"wa": 0, "g2": 512, "v1": 1024, "v2": 1152, "ws": 1664, "lwa": 2176, "lwx": 2688}
NPM = 3200
CS = {"ident": 0, "iu": 128, "iuT": 256, "iue": 384, "blk": 512, "reset": 640}
NCS = 640 + 512


def host_consts():
    c = np.zeros((128, NCS), np.float32)
    i = np.arange(128)
    c[:, 0:128] = np.eye(128)
    c[:, 128:256] = (i[:, None] < i[None, :])
    c[:, 256:384] = (i[None, :] < i[:, None])
    c[:, 384:512] = (i[:, None] <= i[None, :])
    c[:, 512:640] = (i[:, None] // 64 == i[None, :] // 64)
    r = np.ones(512, np.float32)
    r[0::128] = 0.0
    c[:, 640:1152] = r[None, :]
    return c


def host_params(inp):
    pv = np.zeros((2, 128, NPV), np.float32)
    pm = np.zeros((2, 128, NPM), np.float32)
    pb = np.zeros((2, 128, 1536), np.float32)
    col = lambda v: np.ascontiguousarray(np.asarray(v, np.float32).reshape(-1, 128).T)
    for l in range(2):
        def put(name, v):
            a = col(v)
            pv[l, :, PV[name]:PV[name] + a.shape[1]] = a
        put("mu", inp["rwkv_mu"][l])
        put("ln_g", inp["ln_g"][l])
        put("ln_b", inp["ln_b"][l])
        put("gate_b", inp["gate_b"][l])
        put("w0", inp["rwkv_w0"][l])
        put("a0", inp["rwkv_a0"][l])
        put("k_k", inp["rwkv_k_k"][l])
        put("k_a", inp["rwkv_k_a"][l])
        put("r_k", inp["rwkv_r_k"][l])
        put("gn_g", inp["rwkv_gn_g"][l])
        put("gn_b", inp["rwkv_gn_b"][l])
        if l == 1:
            put("v0", inp["rwkv_v0"][0])
        put("cw", inp["lru_conv_w"][l])
        put("cb", inp["lru_conv_b"][l])
        put("ba", inp["lru_ba"][l])
        put("bx", inp["lru_bx"][l])
        put("lam", inp["lru_lam"][l])
        pm[l, 0:64, 0:512] = inp["rwkv_w2"][l]
        pm[l, 64:128, 0:512] = inp["rwkv_a2"][l]
        pm[l, :, 512:1024] = inp["rwkv_g2"][l]
        if l == 1:
            pm[l, :, 1024:1152] = np.asarray(inp["rwkv_v1"][0]).reshape(4, 128, 32).transpose(1, 0, 2).reshape(128, 128)
            pm[l, 0:32, 1152:1664] = inp["rwkv_v2"][0]
        pm[l, :, 1664:2176] = np.asarray(inp["gmlp_ws"][l]).transpose(2, 0, 1).reshape(128, 512)
        for nm, key in (("lwa", "lru_wa"), ("lwx", "lru_wx")):
            wv = np.asarray(inp[key][l])
            for c in range(4):
                for h in range(2):
                    pm[l, 64 * h:64 * h + 64, PM[nm] + c * 128 + 64 * h:PM[nm] + c * 128 + 64 * h + 64] = wv[2 * c + h]
        pb[l, :, 0:512] = np.asarray(inp["gmlp_ln_g"][l])[None, :]
        pb[l, :, 512:1024] = np.asarray(inp["gmlp_ln_b"][l])[None, :]
        pb[l, :, 1024:1536] = np.asarray(inp["gmlp_sb"][l]).reshape(-1)[None, :]
    return pv, pm, pb


def build(SEQ):
    NT = SEQ // T
    nc = bass.Bass("TRN2", target_bir_lowering=False)
    dt_in = lambda n, s: nc.dram_tensor(n, s, F32, kind="ExternalInput").ap()
    x_d = dt_in("x", [SEQ, D])
    w1_d = dt_in("ffn_w1", [2, 2, D, DFF])
    w3_d = dt_in("ffn_w3", [2, 2, D, DFF])
    w2_d = dt_in("ffn_w2", [2, 2, DFF, D])
    win_d = dt_in("w_in", [2, D, NIN])
    pb_d = dt_in("p_branch", [2, 3, 512, D])
    wo_d = dt_in("w_out", [2, D, D])
    pv_d = dt_in("pvec", [2, 128, NPV])
    pm_d = dt_in("pmat", [2, 128, NPM])
    pbc_d = dt_in("pbc", [2, 128, 1536])
    cs_d = dt_in("cst", [128, NCS])
    out_d = nc.dram_tensor("out", [SEQ, D], F32, kind="ExternalOutput").ap()
    import os
    DBG = os.environ.get("KDBG")
    ACTCOPY = os.environ.get("KACT", "")
    NTAP = 200
    dbg_d = nc.dram_tensor("dbg", [NTAP, 128, T], F32, kind="ExternalOutput").ap() if DBG else None
    TAPS.clear()
    tap_bufs = []
    tap_on = [False]
    sc = lambda n, s: nc.dram_tensor(n, s, BF16, kind="Internal").ap()
    w1_s = sc("w1_s", [2, 2, D, DFF])
    w3_s = sc("w3_s", [2, 2, D, DFF])
    w2_s = sc("w2_s", [2, 2, DFF, D])
    win_s = sc("win_s", [2, D, NIN])
    pb_s = sc("pb_s", [2, 3, 512, D])
    wo_s = sc("wo_s", [2, D, D])

    S = Sched(nc)
    es = ExitStack()
    sbuf = lambda n, s, d: es.enter_context(nc.sbuf_tensor(n, s, d))

    N32, N16, NS = 38, 40, 5
    p32_t = sbuf("p32", [128, N32, T], F32)
    p16_t = sbuf("p16", [128, N16, T], BF16)
    P32 = Pool([Tile(p32_t[:, i, :], Buf("p32_%d" % i)) for i in range(N32)])
    P16 = Pool([Tile(p16_t[:, i, :], Buf("p16_%d" % i)) for i in range(N16)])
    ws_t = sbuf("wslots", [128, NS, 4096], BF16)
    ws_b = [Buf("ws%d" % i) for i in range(NS)]
    ws_i = [0]
    ps_t = [es.enter_context(nc.psum_tensor("ps%d" % i, [128, T], F32)) for i in range(8)]
    PS = [Tile(ps_t[i][:], Buf("ps%d" % i)) for i in range(8)]
    ps_i = [0]

    ps_held = set()

    def psget(hold=False):
        while (ps_i[0] % 8) in ps_held:
            ps_i[0] += 1
        i = ps_i[0] % 8
        ps_i[0] += 1
        if hold:
            ps_held.add(i)
        return PS[i]

    def psrel(*ts_):
        for t_ in ts_:
            ps_held.discard(PS.index(t_))

    cst = sbuf("cst32", [128, NCS], F32)
    cst_b = Buf("cst")
    cbf = sbuf("cstbf", [128, 640], BF16)
    cbf_b = Buf("cbf")
    mNN = sbuf("mNN", [128, 512], BF16)
    mQ = sbuf("mQ", [128, 384], BF16)
    mBR = sbuf("mBR", [128, 512], BF16)
    msk_b = Buf("msk")
    onesm = sbuf("onesm", [128, 128], BF16)
    blkmean = sbuf("blkmean", [128, 128], BF16)
    zeros16 = sbuf("zeros16", [128, 128], BF16)
    pv = sbuf("pv", [128, 2, NPV], F32)
    pv_b = Buf("pv")
    pm = sbuf("pm", [128, 2, NPM], BF16)
    pm_b = Buf("pm")
    pbc = sbuf("pbc_sb", [128, 2, 1536], F32)
    pbc_b = Buf("pbc")
    zprev = sbuf("zprev", [128, 2, 14], F32)
    zprev_b = [[Buf() for _ in range(14)] for _ in range(2)]
    Sbf = sbuf("Sbf", [128, 2, 4, 128], BF16)
    Sbf_b = [[Buf() for _ in range(4)] for _ in range(2)]
    hprev = sbuf("hprev", [128, 2, 4], F32)
    hprev_b = [[Buf() for _ in range(4)] for _ in range(2)]
    halo = sbuf("halo", [128, 2, 4, 3], F32)
    halo_b = [[Buf() for _ in range(4)] for _ in range(2)]
    zxb = sbuf("zxb", [128, 4, 515], F32)
    zxb_b = [Buf() for _ in range(4)]
    small = sbuf("small", [128, 16], F32)
    small_b = Buf("small")

    ident32 = cst[:, 0:128]
    identb = cbf[:, 0:128]
    blkb = cbf[:, 512:640]
    iueb = cbf[:, 384:512]

    def bl(xs):
        return [x.b if isinstance(x, Tile) else x for x in xs]

    def act(out, in_, func, r, w, bias=None, scale=None):
        if func == AF.Identity:
            if bias is None and scale is None:
                if ACTCOPY == "copy":
                    S.op("scalar", lambda e: e.copy(out=out, in_=in_), bl(r), bl(w))
                elif ACTCOPY == "lrelu":
                    S.op("scalar", lambda e: e.activation(out=out, in_=in_, func=AF.Lrelu, alpha=1.0), bl(r), bl(w))
                elif ACTCOPY == "relu":
                    S.op("scalar", lambda e: e.activation(out=out, in_=in_, func=AF.Relu), bl(r), bl(w))
                else:
                    S.op("vector", lambda e: e.tensor_copy(out=out, in_=in_), bl(r), bl(w))
            else:
                S.op("vector", lambda e: e.tensor_scalar(out=out, in0=in_, scalar1=scale, scalar2=bias, op0=ALU.mult, op1=ALU.add), bl(r), bl(w))
            return
        kw = {}
        if bias is not None:
            kw["bias"] = bias
        if scale is not None:
            kw["scale"] = scale
        S.op("scalar", lambda e: e.activation(out=out, in_=in_, func=func, **kw), bl(r), bl(w))

    def tt(eng, out, in0, in1, op, r, w):
        S.op(eng, lambda e: e.tensor_tensor(out=out, in0=in0, in1=in1, op=op), bl(r), bl(w))

    def stt(out, in0, scalar, in1, op0, op1, r, w):
        S.op("vector", lambda e: e.scalar_tensor_tensor(out=out, in0=in0, scalar=scalar, in1=in1, op0=op0, op1=op1), bl(r), bl(w))

    def ts(out, in0, s1, s2, op0, op1, r, w, eng="vector"):
        S.op(eng, lambda e: e.tensor_scalar(out=out, in0=in0, scalar1=s1, scalar2=s2, op0=op0, op1=op1), bl(r), bl(w))

    def ts1(out, in0, s1, op0, r, w, eng="vector"):
        S.op(eng, lambda e: e.tensor_single_scalar(out=out, in_=in0, scalar=s1, op=op0), bl(r), bl(w))

    def cp(eng, out, in_, r, w):
        if eng == "scalar":
            S.op(eng, lambda e: e.copy(out=out, in_=in_), bl(r), bl(w))
        else:
            S.op(eng, lambda e: e.tensor_copy(out=out, in_=in_), bl(r), bl(w))

    def mm(out, lhsT, rhs, start, stop, r, w, tp=None):
        if tp is None:
            S.op("tensor", lambda e: e.matmul(out, lhsT=lhsT, rhs=rhs, start=start, stop=stop), bl(r), bl(w))
        else:
            S.op("tensor", lambda e: e.matmul(out, lhsT=lhsT, rhs=rhs, start=start, stop=stop, tile_position=tp), bl(r), bl(w))

    def tr(out, in_, ident, r, w):
        S.op("tensor", lambda e: e.matmul(out, lhsT=in_, rhs=ident, start=True, stop=True), bl(r), bl(w))

    def dma(eng, out, in_, r, w, dsem):
        S.op(eng, lambda e: e.dma_start(out=out, in_=in_), bl(r), bl(w), dma=True, dsem=dsem)

    def memset(eng, ap, val, w):
        S.op(eng, lambda e: e.memset(ap, val), [], bl(w))

    def tap(name, t_, bf=False):
        if not (DBG and tap_on[0]):
            return
        i = len(TAPS)
        TAPS.append(name)
        src = t_
        if bf:
            src = P32.get()
            S.op("vector", lambda e: e.tensor_copy(out=src.ap, in_=t_.ap), [t_.b], [src.b])
        ob = Buf()
        dma("scalar", dbg_d[i], src.ap, [src], [ob], "tap%d" % (i % 4))
        tap_bufs.append(ob)
        if bf:
            P32.put(src)

    def wload(src, shape, srcbuf):
        i = ws_i[0] % NS
        ws_i[0] += 1
        n = shape[0] * shape[1]
        view = ws_t[:, i, 0:n].rearrange("p (a b) -> p a b", b=shape[1])
        dma("sync", view, src, list(cast_last.values()), [ws_b[i]], "ws%d" % i)
        return view, ws_b[i]

    dma("sync", cst[:], cs_d, [], [cst_b], "cst")
    dma("sync", pv[:, 0, :], pv_d[0], [], [pv_b], "cst")
    dma("sync", pv[:, 1, :], pv_d[1], [], [pv_b], "cst")
    dma("sync", pbc[:, 0, :], pbc_d[0], [], [pbc_b], "cst")
    dma("sync", pbc[:, 1, :], pbc_d[1], [], [pbc_b], "cst")
    cp("vector", cbf[:], cst[:, 0:640], [cst_b], [cbf_b])
    for h in range(2):
        cp("vector", mNN[:, h * 128:(h + 1) * 128], cst[:, 128:256], [cst_b], [msk_b])
        cp("vector", mNN[:, 256 + h * 128:256 + (h + 1) * 128], cst[:, 256:384], [cst_b], [msk_b])
        memset("vector", mQ[:, h * 192:h * 192 + 64], 1.0, [msk_b])
        cp("vector", mQ[:, h * 192 + 64:(h + 1) * 192], cst[:, 256:384], [cst_b], [msk_b])
    for q in range(4):
        cp("vector", mBR[:, q * 128:(q + 1) * 128], cst[:, 384:512], [cst_b], [msk_b])
    memset("vector", onesm[:], 1.0 / 1024.0, [msk_b])
    ts1(blkmean[:], cst[:, 512:640], 1.0 / 64.0, ALU.mult, [cst_b], [msk_b])
    memset("vector", zeros16[:], 0.0, [msk_b])
    memset("vector", zprev[:], 0.0, [b for l in zprev_b for b in l])
    memset("vector", Sbf[:], 0.0, [b for l in Sbf_b for b in l])
    memset("vector", hprev[:], 0.0, [b for l in hprev_b for b in l])
    memset("vector", halo[:], 0.0, [b for l in halo_b for b in l])
    CS32 = [P32.get() for _ in range(8)]
    CS16 = [P16.get() for _ in range(8)]
    cast_i = [0]
    for l in range(2):
        for c0 in range(0, NPM, 512):
            n = min(512, NPM - c0)
            i = cast_i[0]
            cast_i[0] += 1
            st = CS32[i % 8]
            dma("sync", st.ap[:, 0:n], pm_d[l][:, c0:c0 + n], [], [st], "cl%d" % (i % 8))
            cp("vector", pm[:, l, c0:c0 + n], st.ap[:, 0:n], [st], [pm_b])
        for g in range(4):
            wsl = pm[:, l, PM["ws"] + g * 128:PM["ws"] + (g + 1) * 128]
            tt("vector", wsl, wsl, cbf[:, 384:512], ALU.mult, [pm_b, cbf_b], [pm_b])
        o = PV
        ts(pv[:, l, o["omu"]:o["omu"] + 14], pv[:, l, o["mu"]:o["mu"] + 14], -1.0, 1.0, ALU.mult, ALU.add, [pv_b], [pv_b])
        ts(pv[:, l, o["oka"]:o["oka"] + 4], pv[:, l, o["k_a"]:o["k_a"] + 4], -1.0, 1.0, ALU.mult, ALU.add, [pv_b], [pv_b])
        act(small[:, 0:4], pv[:, l, o["lam"]:o["lam"] + 4], AF.Exp, [pv_b], [small_b], scale=-1.0)
        act(small[:, 4:8], small[:, 0:4], AF.Ln, [small_b], [small_b], bias=1.0)
        ts1(pv[:, l, o["clam"]:o["clam"] + 4], small[:, 4:8], -8.0, ALU.mult, [small_b], [pv_b])
        ts1(pv[:, l, o["clam2"]:o["clam2"] + 4], small[:, 4:8], -16.0, ALU.mult, [small_b], [pv_b])

    sb = {}

    cast_last = {}

    def cast(name, dst, src, rows):
        ncol = src.shape[1]
        for r0 in range(0, rows, 128):
            for c0 in range(0, ncol, 512):
                n = min(512, ncol - c0)
                i = cast_i[0]
                cast_i[0] += 1
                a, b16 = CS32[i % 8], CS16[i % 8]
                dma("sync", a.ap[:, 0:n], src[r0:r0 + 128, c0:c0 + n], [], [a], "cl%d" % (i % 8))
                cp("vector" if i % 2 == 0 else "gpsimd", b16.ap[:, 0:n], a.ap[:, 0:n], [a], [b16])
                ob = Buf()
                dma("scalar", dst[r0:r0 + 128, c0:c0 + n], b16.ap[:, 0:n], [b16], [ob], "cs%d" % (i % 8))
                cast_last[i % 8] = ob
        return None

    for l in range(2):
        for f in range(2):
            sb["w1", l, f] = cast("w1%d%d" % (l, f), w1_s[l, f], w1_d[l, f], D)
            sb["w3", l, f] = cast("w3%d%d" % (l, f), w3_s[l, f], w3_d[l, f], D)
            sb["w2", l, f] = cast("w2%d%d" % (l, f), w2_s[l, f], w2_d[l, f], DFF)
            if f == 0:
                sb["win", l] = cast("win%d" % l, win_s[l], win_d[l], D)
                for b in range(3):
                    sb["pb", l, b] = cast("pb%d%d" % (l, b), pb_s[l, b], pb_d[l, b], 512)
                sb["wo", l] = cast("wo%d" % l, wo_s[l], wo_d[l], D)

    P32.put(*CS32)
    P16.put(*CS16)

    def pcol(l, name, j):
        return pv[:, l, PV[name] + j:PV[name] + j + 1]

    def layer_norm(l, idx, x32, xbf, contrib, coef):
        pm_, pq_ = psget(True), psget(True)
        for c in range(KC):
            pt = contrib(c)
            if pt is not None:
                stt(x32[c].ap, pt.ap, coef, x32[c].ap, ALU.mult, ALU.add, [pt, x32[c]], [x32[c]])
            yb, yq = P16.get(), P16.get()
            cp("vector", yb.ap, x32[c].ap, [x32[c]], [yb])
            tt("gpsimd", yq.ap, x32[c].ap, x32[c].ap, ALU.mult, [x32[c]], [yq])
            mm(pm_.ap, onesm[:], yb.ap, c == 0, c == KC - 1, [msk_b, yb], [pm_])
            mm(pq_.ap, onesm[:], yq.ap, c == 0, c == KC - 1, [msk_b, yq], [pq_])
            P16.put(yb, yq)
        mean, rstd = P32.get(), P32.get()
        act(mean.ap, pm_.ap, AF.Identity, [pm_], [mean])
        tt("gpsimd", rstd.ap, mean.ap, mean.ap, ALU.mult, [mean], [rstd])
        tt("vector", rstd.ap, pq_.ap, rstd.ap, ALU.subtract, [pq_, rstd], [rstd])
        act(rstd.ap, rstd.ap, AF.Ln, [rstd], [rstd], bias=LN_EPS)
        act(rstd.ap, rstd.ap, AF.Exp, [rstd], [rstd], scale=-0.5)
        for c in range(KC):
            tmp = P32.get()
            tt("gpsimd", tmp.ap, x32[c].ap, mean.ap, ALU.subtract, [x32[c], mean], [tmp])
            tt("vector", tmp.ap, tmp.ap, rstd.ap, ALU.mult, [tmp, rstd], [tmp])
            g_ = pcol(l, "ln_g", idx * 8 + c)
            b_ = pcol(l, "ln_b", idx * 8 + c)
            ts(x32[c].ap, tmp.ap, g_, b_, ALU.mult, ALU.add, [tmp, pv_b], [x32[c]])
            cp("vector", xbf[c].ap, x32[c].ap, [x32[c]], [xbf[c]])
            P32.put(tmp)
        P32.put(mean, rstd)
        psrel(pm_, pq_)

    def ffn(l, f, x32, xbf):
        g = [P16.get() for _ in range(MC)]
        for gi in range(6):
            c0 = gi * 512
            n = min(512, DFF - c0)
            w1v, w1b = wload(w1_s[l, f].rearrange("(k p) n -> p k n", p=128)[:, :, c0:c0 + n], [KC, n], sb["w1", l, f])
            w3v, w3b = wload(w3_s[l, f].rearrange("(k p) n -> p k n", p=128)[:, :, c0:c0 + n], [KC, n], sb["w3", l, f])
            for pr in range(0, n // 128, 2):
                mis = [pr, pr + 1]
                pa = {mi: psget() for mi in mis}
                pb_ = {mi: psget() for mi in mis}
                for k in range(KC):
                    for mi in mis:
                        mm(pa[mi].ap, w1v[:, k, mi * 128:(mi + 1) * 128], xbf[k].ap, k == 0, k == KC - 1, [w1b, xbf[k]], [pa[mi]])
                        mm(pb_[mi].ap, w3v[:, k, mi * 128:(mi + 1) * 128], xbf[k].ap, k == 0, k == KC - 1, [w3b, xbf[k]], [pb_[mi]])
                for mi in mis:
                    m = gi * 4 + mi
                    s = P32.get()
                    act(s.ap, pa[mi].ap, AF.Silu, [pa[mi]], [s])
                    tt("vector", g[m].ap, s.ap, pb_[mi].ap, ALU.mult, [s, pb_[mi]], [g[m]])
                    P32.put(s)
        accs = {}
        for half in range(2):
            pacc = [psget(True) for _ in range(4)]
            for mg in range(3):
                m0 = mg * 8
                nm = min(8, MC - m0)
                wv, wb = wload(w2_s[l, f].rearrange("(m p) n -> p m n", p=128)[:, m0:m0 + nm, half * 512:(half + 1) * 512], [nm, 512], sb["w2", l, f])
                order = [(mi, j) for mi in range(nm) for j in range(4)]
                if half == 1 and mg == 2:
                    order = [(mi, j) for j in range(4) for mi in range(nm)]
                for mi, j in order:
                    m = m0 + mi
                    mm(pacc[j].ap, wv[:, mi, j * 128:(j + 1) * 128], g[m].ap, m == 0, m == MC - 1, [wb, g[m]], [pacc[j]])
            for j in range(4):
                accs[half * 4 + j] = pacc[j]
            if half == 0:
                for j in range(4):
                    stt(x32[j].ap, pacc[j].ap, C_FFN, x32[j].ap, ALU.mult, ALU.add, [pacc[j], x32[j]], [x32[j]])
                    accs[j] = None
                psrel(*pacc)
        P16.put(*g)
        layer_norm(l, 0 if f == 0 else 2, x32, xbf, lambda c: accs[c], C_FFN)
        psrel(*pacc)

    def mixer(l, x32, xbf, vfirst):
        o = PV
        winv = win_s[l].rearrange("(k p) n -> p k n", p=128)

        def inproj_fm(wv, wb, mi):
            p = psget()
            for k in range(KC):
                mm(p.ap, wv[:, k, mi * 128:(mi + 1) * 128], xbf[k].ap, k == 0, k == KC - 1, [wb, xbf[k]], [p])
            return p

        zr = []
        for gi in range(4):
            c0 = gi * 512
            n = 512 if gi < 3 else 256
            wv, wb = wload(winv[:, :, c0:c0 + n], [KC, n], sb["win", l])
            nmi = n // 128
            pgs = [psget() for _ in range(nmi)]
            for k in range(KC):
                for mi in range(nmi):
                    mm(pgs[mi].ap, wv[:, k, mi * 128:(mi + 1) * 128], xbf[k].ap, k == 0, k == KC - 1, [wb, xbf[k]], [pgs[mi]])
            for mi in range(nmi):
                m = gi * 4 + mi
                p = pgs[mi]
                dd = P32.get()
                ts1(dd.ap, p.ap, pcol(l, "omu", m), ALU.mult, [p, pv_b], [dd])
                stt(dd.ap[:, 1:T], p.ap[:, 0:T - 1], pcol(l, "mu", m), dd.ap[:, 1:T], ALU.mult, ALU.add, [p, dd, pv_b], [dd])
                stt(dd.ap[:, 0:1], zprev[:, l, m:m + 1], pcol(l, "mu", m), dd.ap[:, 0:1], ALU.mult, ALU.add, [zprev_b[l][m], dd, pv_b], [dd])
                cp("vector", zprev[:, l, m:m + 1], p.ap[:, T - 1:T], [p], [zprev_b[l][m]])
                zr.append(dd)
                tap("zr_%d:%d" % (l, m), dd)
        wain, sgin = P16.get(), P16.get()
        act(wain.ap[0:64, :], zr[12].ap[0:64, :], AF.Tanh, [zr[12]], [wain])
        act(wain.ap[64:128, :], zr[12].ap[64:128, :], AF.Identity, [zr[12]], [wain])
        act(sgin.ap, zr[13].ap, AF.Sigmoid, [zr[13]], [sgin])
        P32.put(zr[12], zr[13])
        pmv = lambda name, c0, n: pm[:, l, PM[name] + c0:PM[name] + c0 + n]
        lo_bf = None
        if l == 1:
            vb = []
            plo = psget(True)
            for hp in range(4):
                t_ = P16.get()
                cp("vector", t_.ap, zr[8 + hp].ap, [zr[8 + hp]], [t_])
                vb.append(t_)
                mm(plo.ap[0:32, :], pmv("v1", hp * 32, 32), t_.ap, hp == 0, hp == 3, [pm_b, t_], [plo])
            lo_bf = P16.get()
            cp("vector", lo_bf.ap[0:32, :], plo.ap[0:32, :], [plo], [lo_bf])
            psrel(plo)
            P16.put(*vb)
        o_rwkv = []

        def pre(hp):
            r32, k32, v32 = zr[hp], zr[4 + hp], zr[8 + hp]
            if l == 0:
                vfirst.append(v32)
            else:
                p = psget()
                mm(p.ap, pm[0:32, l, PM["v2"] + hp * 128:PM["v2"] + (hp + 1) * 128], lo_bf.ap[0:32, :], True, True, [pm_b, lo_bf], [p])
                sg_ = P32.get()
                act(sg_.ap, p.ap, AF.Sigmoid, [p, pv_b], [sg_], bias=pcol(l, "v0", hp))
                yield
                dv = P32.get()
                tt("gpsimd", dv.ap, vfirst[hp].ap, v32.ap, ALU.subtract, [vfirst[hp], v32], [dv])
                tt("vector", dv.ap, dv.ap, sg_.ap, ALU.mult, [dv, sg_], [dv])
                tt("gpsimd", v32.ap, v32.ap, dv.ap, ALU.add, [v32, dv], [v32])
                P32.put(sg_, dv, vfirst[hp])
            p = psget()
            mm(p.ap, pm[0:64, l, PM["wa"] + hp * 128:PM["wa"] + (hp + 1) * 128], wain.ap[0:64, :], True, True, [pm_b, wain], [p])
            lw = P32.get()
            act(lw.ap, p.ap, AF.Sigmoid, [p, pv_b], [lw], bias=pcol(l, "w0", hp))
            yield
            p = psget()
            mm(p.ap, pm[64:128, l, PM["wa"] + hp * 128:PM["wa"] + (hp + 1) * 128], wain.ap[64:128, :], True, True, [pm_b, wain], [p])
            a32 = P32.get()
            act(a32.ap, p.ap, AF.Sigmoid, [p, pv_b], [a32], bias=pcol(l, "a0", hp))
            yield
            p = psget()
            mm(p.ap, pmv("g2", hp * 128, 128), sgin.ap, True, True, [pm_b, sgin], [p])
            g32 = P32.get()
            act(g32.ap, p.ap, AF.Identity, [p], [g32])
            yield
            kk = P32.get()
            ts1(kk.ap, k32.ap, pcol(l, "k_k", hp), ALU.mult, [k32, pv_b], [kk])
            kq = P16.get()
            tt("gpsimd", kq.ap, kk.ap, kk.ap, ALU.mult, [kk], [kq])
            p = psget()
            mm(p.ap, blkb, kq.ap, True, True, [cbf_b, kq], [p])
            P16.put(kq)
            nr = P32.get()
            act(nr.ap, p.ap, AF.Ln, [p], [nr], bias=1e-24)
            yield
            act(nr.ap, nr.ap, AF.Exp, [nr], [nr], scale=-0.5)
            tt("gpsimd", kk.ap, kk.ap, nr.ap, ALU.mult, [kk, nr], [kk])
            P32.put(nr)
            kf = P32.get()
            ts(kf.ap, a32.ap, pcol(l, "k_a", hp), pcol(l, "oka", hp), ALU.mult, ALU.add, [a32, pv_b], [kf])
            tt("vector", kf.ap, kf.ap, k32.ap, ALU.mult, [kf, k32], [kf])
            P32.put(k32)
            rk = P16.get()
            stt(rk.ap, r32.ap, pcol(l, "r_k", hp), kf.ap, ALU.mult, ALU.mult, [r32, kf, pv_b], [rk])
            p = psget()
            mm(p.ap, blkb, rk.ap, True, True, [cbf_b, rk], [p])
            P16.put(rk)
            bon = P32.get()
            tt("vector", bon.ap, p.ap, v32.ap, ALU.mult, [p, v32], [bon])
            yield
            L, Lm = P32.get(), P32.get()
            S.op("vector", lambda e, L=L, lw=lw: e.tensor_tensor_scan(out=L.ap, data0=cst[:, 640:1152], data1=lw.ap, initial=0.0, op0=ALU.mult, op1=ALU.add), [cst_b, lw.b], [L.b])
            tt("gpsimd", Lm.ap, L.ap, lw.ap, ALU.subtract, [L, lw], [Lm])
            tap("lw_%d:%d" % (l, hp), lw)
            tap("a_%d:%d" % (l, hp), a32)
            P32.put(lw)
            eL, enL, eC = P32.get(), P32.get(), P32.get()
            act(eL.ap, L.ap, AF.Exp, [L], [eL], scale=NEG_E)
            act(Lm.ap, Lm.ap, AF.Exp, [Lm], [Lm], scale=NEG_E)
            act(enL.ap, L.ap, AF.Exp, [L], [enL], scale=-NEG_E)
            L3 = L.ap.rearrange("p (c t) -> p c t", t=128)
            S.op("vector", lambda e, L3=L3, eC=eC: e.tensor_tensor(out=eC.ap.rearrange("p (c t) -> p c t", t=128), in0=L3[:, :, 127:128].broadcast_to([128, 4, 128]), in1=L3, op=ALU.subtract), [L.b], [eC.b])
            act(eC.ap, eC.ap, AF.Exp, [eC], [eC], scale=NEG_E)
            yield
            P32.put(L)
            bb = P32.get()
            tt("gpsimd", bb.ap, kk.ap, a32.ap, ALU.mult, [kk, a32], [bb])
            P32.put(a32)
            Ap, Rp, Bm, Km, BC, KCc, vbf = [P16.get() for _ in range(7)]
            stt(Ap.ap, kk.ap, -1.0, Lm.ap, ALU.mult, ALU.mult, [kk, Lm], [Ap])
            yield
            tt("vector", Rp.ap, r32.ap, eL.ap, ALU.mult, [r32, eL], [Rp])
            tt("gpsimd", Bm.ap, bb.ap, enL.ap, ALU.mult, [bb, enL], [Bm])
            tt("gpsimd", Km.ap, kf.ap, enL.ap, ALU.mult, [kf, enL], [Km])
            yield
            tt("gpsimd", BC.ap, bb.ap, eC.ap, ALU.mult, [bb, eC], [BC])
            tt("gpsimd", KCc.ap, kf.ap, eC.ap, ALU.mult, [kf, eC], [KCc])
            cp("vector", vbf.ap, v32.ap, [v32], [vbf])
            tap("kk_%d:%d" % (l, hp), kk)
            tap("kf_%d:%d" % (l, hp), kf)
            tap("v_%d:%d" % (l, hp), v32)
            tap("g_%d:%d" % (l, hp), g32)
            P32.put(kk, Lm, enL, eC, bb, kf, r32)
            if l == 1:
                P32.put(v32)
            yield
            return (Ap, Rp, Bm, Km, BC, KCc, vbf, eL, bon, g32)

        gens, pres = {}, {}

        def advance(hq, n):
            if hq > 3 or hq in pres:
                return
            if hq not in gens:
                gens[hq] = pre(hq)
            try:
                for _ in range(n):
                    next(gens[hq])
            except StopIteration as e_:
                pres[hq] = e_.value

        for hp in range(4):
            advance(hp, 10 ** 6)
            Ap, Rp, Bm, Km, BC, KCc, vbf, eL, bon, g32 = pres[hp]
            pO = psget(True)
            units = [dict(NN=[P16.get(), P16.get()], Q=P16.get(), BR=P16.get(), TT=P16.get()) for _ in range(2)]
            Sv = Sbf[:, l, hp, :]
            Sb_ = Sbf_b[l][hp]
            for pair in range(2):
                for ui in range(2):
                    u = units[ui]
                    c = pair * 2 + ui
                    u["c"] = c
                    cs = slice(c * 128, (c + 1) * 128)
                    u["cs"] = cs
                    p0, p1, p2, p3 = psget(), psget(), psget(), psget()
                    for h in range(2):
                        hs = slice(64 * h, 64 * h + 64)
                        rd = [Ap, Bm, Km, Rp, BC, vbf, cbf_b]
                        mm(p0.ap[:, h * 128:(h + 1) * 128], Bm.ap[hs, cs], Ap.ap[hs, cs], True, True, rd, [p0])
                        mm(p0.ap[:, 256 + h * 128:256 + (h + 1) * 128], Ap.ap[hs, cs], Bm.ap[hs, cs], True, True, rd, [p0])
                        mm(p1.ap[:, h * 192:h * 192 + 64], Ap.ap[hs, cs], cbf[hs, 64 * h:64 * h + 64], True, True, rd, [p1])
                        mm(p1.ap[:, h * 192 + 64:(h + 1) * 192], Ap.ap[hs, cs], Km.ap[hs, cs], True, True, rd, [p1])
                        mm(p2.ap[:, h * 128:(h + 1) * 128], Bm.ap[hs, cs], Rp.ap[hs, cs], True, True, rd, [p2])
                        mm(p2.ap[:, 256 + h * 128:256 + (h + 1) * 128], Km.ap[hs, cs], Rp.ap[hs, cs], True, True, rd, [p2])
                        mm(p3.ap[:, h * 64:(h + 1) * 64], BC.ap[hs, cs], cbf[hs, 64 * h:64 * h + 64], True, True, rd, [p3])
                        mm(p3.ap[:, 128 + h * 64:128 + (h + 1) * 64], vbf.ap[hs, cs], cbf[hs, 64 * h:64 * h + 64], True, True, rd, [p3])
                    u["cur"] = 0
                    NN, Q, BR, TTt = u["NN"], u["Q"], u["BR"], u["TT"]
                    tt("vector", NN[0].ap, p0.ap, mNN[:], ALU.mult, [p0, msk_b], [NN[0]])
                    tt("vector", Q.ap[:, 0:384], p1.ap[:, 0:384], mQ[:], ALU.mult, [p1, msk_b], [Q])
                    tt("vector", BR.ap, p2.ap, mBR[:], ALU.mult, [p2, msk_b], [BR])
                    act(TTt.ap[:, 0:256], p3.ap[:, 0:256], AF.Identity, [p3], [TTt])
                for lev in range(7):
                    pend = []
                    for u in units:
                        NN, Q = u["NN"], u["Q"]
                        N_ = NN[u["cur"]]
                        pq = psget()
                        for h in range(2):
                            mm(pq.ap[:, h * 192:(h + 1) * 192], N_.ap[:, h * 128:(h + 1) * 128], Q.ap[:, h * 192:(h + 1) * 192], True, True, [N_, Q], [pq])
                        pn = None
                        if lev < 6:
                            pn = psget()
                            for h in range(2):
                                mm(pn.ap[:, h * 128:(h + 1) * 128], N_.ap[:, 256 + h * 128:256 + (h + 1) * 128], N_.ap[:, h * 128:(h + 1) * 128], True, True, [N_], [pn])
                                if lev < 5:
                                    mm(pn.ap[:, 256 + h * 128:256 + (h + 1) * 128], N_.ap[:, h * 128:(h + 1) * 128], N_.ap[:, 256 + h * 128:256 + (h + 1) * 128], True, True, [N_], [pn])
                        pend.append((pq, pn))
                    for u, (pq, pn) in zip(units, pend):
                        NN, Q = u["NN"], u["Q"]
                        if lev < 6:
                            wd = 512 if lev < 5 else 256
                            act(NN[1 - u["cur"]].ap[:, 0:wd], pn.ap[:, 0:wd], AF.Identity, [pn], [NN[1 - u["cur"]]])
                            u["cur"] = 1 - u["cur"]
                        tt("vector", Q.ap[:, 0:384], pq.ap[:, 0:384], Q.ap[:, 0:384], ALU.add, [pq, Q], [Q])
                    advance(hp + 1, 1)
                for u in units:
                    NN, Q, BR, TTt, c, cs = u["NN"], u["Q"], u["BR"], u["TT"], u["c"], u["cs"]
                    XH = NN[u["cur"]]
                    YY = NN[1 - u["cur"]]
                    u["XH"], u["YY"] = XH, YY
                    pd, py = psget(), psget()
                    for h in range(2):
                        mm(pd.ap[64 * h:64 * h + 64, 0:128], Q.ap[:, h * 192:h * 192 + 64], BR.ap[:, h * 128:(h + 1) * 128], True, True, [Q, BR], [pd], tp=(0, 64 * h))
                        mm(pd.ap[:, 128 + h * 128:256 + h * 128], Q.ap[:, h * 192 + 64:(h + 1) * 192], BR.ap[:, h * 128:(h + 1) * 128], True, True, [Q, BR], [pd])
                    mm(py.ap[:, 0:128], zeros16[:], identb, True, False, [msk_b, cbf_b], [py])
                    for h in range(2):
                        mm(py.ap[64 * h:64 * h + 64, 64 * h:64 * h + 64], Q.ap[:, h * 192:h * 192 + 64], TTt.ap[:, h * 64:(h + 1) * 64], False, False, [Q, TTt], [py], tp=(0, 64 * h))
                    mm(py.ap[:, 0:128], zeros16[:], identb, False, True, [msk_b, cbf_b], [py])
                    for h in range(2):
                        hs = slice(64 * h, 64 * h + 64)
                        mm(py.ap[:, 128 + h * 64:128 + (h + 1) * 64], Q.ap[:, h * 192 + 64:(h + 1) * 192], TTt.ap[:, h * 64:(h + 1) * 64], True, False, [Q, TTt], [py])
                        mm(py.ap[:, 128 + h * 64:128 + (h + 1) * 64], KCc.ap[hs, cs], cbf[hs, 64 * h:64 * h + 64], False, True, [KCc, cbf_b], [py])
                    tt("vector", XH.ap[:, 0:128], pd.ap[:, 0:128], Rp.ap[:, cs], ALU.add, [pd, Rp], [XH])
                    tt("vector", XH.ap[:, 128:384], pd.ap[:, 128:384], BR.ap[:, 256:512], ALU.add, [pd, BR], [XH])
                    stt(YY.ap[:, 0:128], ident32, eL.ap[:, c * 128 + 127:c * 128 + 128], py.ap[:, 0:128], ALU.mult, ALU.add, [cst_b, eL, py], [YY])
                    act(XH.ap[:, 384:512], py.ap[:, 128:256], AF.Identity, [py], [XH])
                for u in units:
                    XH, YY, TTt, cs = u["XH"], u["YY"], u["TT"], u["cs"]
                    for h in range(2):
                        mm(pO.ap[64 * h:64 * h + 64, cs], TTt.ap[:, 128 + h * 64:128 + (h + 1) * 64], XH.ap[:, 128 + h * 128:256 + h * 128], True, False, [TTt, XH], [pO], tp=(0, 64 * h))
                    mm(pO.ap[:, cs], Sv, XH.ap[:, 0:128], False, True, [Sb_, XH], [pO])
                    pS = psget()
                    mm(pS.ap[:, 0:128], YY.ap[:, 0:128], Sv, True, False, [YY, Sb_], [pS])
                    mm(pS.ap[:, 0:128], XH.ap[:, 384:512], TTt.ap[:, 128:256], False, True, [XH, TTt], [pS])
                    tt("vector", Sv, pS.ap[:, 0:128], blkb, ALU.mult, [pS, cbf_b], [Sb_])
            for u in units:
                P16.put(u["NN"][0], u["NN"][1], u["Q"], u["BR"], u["TT"])
            P16.put(Ap, Rp, Bm, Km, BC, KCc, vbf)
            P32.put(eL)
            o32 = P32.get()
            act(o32.ap, pO.ap, AF.Identity, [pO], [o32])
            tap("wkv_%d:%d" % (l, hp), o32)
            psrel(pO)
            ob = P16.get()
            cp("vector", ob.ap, o32.ap, [o32], [ob])
            p = psget()
            mm(p.ap, blkmean[:], ob.ap, True, True, [msk_b, ob], [p])
            tt("vector", o32.ap, o32.ap, p.ap, ALU.subtract, [o32, p], [o32])
            tt("gpsimd", ob.ap, o32.ap, o32.ap, ALU.mult, [o32], [ob])
            p = psget()
            mm(p.ap, blkmean[:], ob.ap, True, True, [msk_b, ob], [p])
            P16.put(ob)
            sd = P32.get()
            act(sd.ap, p.ap, AF.Ln, [p], [sd], bias=GN_EPS)
            act(sd.ap, sd.ap, AF.Exp, [sd], [sd], scale=-0.5)
            tt("vector", o32.ap, o32.ap, sd.ap, ALU.mult, [o32, sd], [o32])
            P32.put(sd)
            ts(o32.ap, o32.ap, pcol(l, "gn_g", hp), pcol(l, "gn_b", hp), ALU.mult, ALU.add, [o32, pv_b], [o32])
            tt("gpsimd", o32.ap, o32.ap, bon.ap, ALU.add, [o32, bon], [o32])
            of = P16.get()
            tt("vector", of.ap, o32.ap, g32.ap, ALU.mult, [o32, g32], [of])
            P32.put(o32, bon, g32)
            o_rwkv.append(of)
            tap("o_rwkv_%d:%d" % (l, hp), of, True)
        P16.put(wain, sgin)
        if l == 1:
            P16.put(lo_bf)
        wv, wb = wload(winv[:, :, 1792:2304], [KC, 512], sb["win", l])
        u32 = []
        for g in range(4):
            p = inproj_fm(wv, wb, g)
            u = P32.get()
            act(u.ap, p.ap, AF.Gelu_apprx_tanh, [p], [u])
            u32.append(u)
        wv, wb = wload(winv[:, :, 2304:2816], [KC, 512], sb["win", l])
        vtm = []
        sm_bs = [Buf() for _ in range(4)]
        for tc in range(4):
            small_b = sm_bs[tc]
            p = psget()
            for k in range(KC):
                mm(p.ap, xbf[k].ap[:, tc * 128:(tc + 1) * 128], wv[:, k, :], k == 0, k == KC - 1, [wb, xbf[k]], [p])
            gv = P32.get()
            act(gv.ap, p.ap, AF.Gelu_apprx_tanh, [p], [gv])
            sm = small[:, 8 + tc:9 + tc]
            S.op("vector", lambda e, gv=gv, sm=sm: e.reduce_sum(out=sm, in_=gv.ap, axis=AX.X), [gv.b], [small_b])
            ts1(sm, sm, -1.0 / 512.0, ALU.mult, [small_b], [small_b])
            ts1(gv.ap, gv.ap, sm, ALU.add, [gv, small_b], [gv])
            sq = P32.get()
            tt("gpsimd", sq.ap, gv.ap, gv.ap, ALU.mult, [gv], [sq])
            sv_ = small[:, 12 + tc:13 + tc]
            S.op("vector", lambda e, sq=sq, sv_=sv_: e.reduce_sum(out=sv_, in_=sq.ap, axis=AX.X), [sq.b], [small_b])
            P32.put(sq)
            act(sv_, sv_, AF.Sqrt, [small_b], [small_b], bias=1e-5, scale=1.0 / 512.0)
            S.op("vector", lambda e, sv_=sv_: e.reciprocal(out=sv_, in_=sv_), [small_b], [small_b])
            ts1(gv.ap, gv.ap, sv_, ALU.mult, [gv, small_b], [gv])
            tt("vector", gv.ap, gv.ap, pbc[:, l, 0:512], ALU.mult, [gv, pbc_b], [gv])
            vt = P16.get()
            tt("vector", vt.ap, gv.ap, pbc[:, l, 512:1024], ALU.add, [gv, pbc_b], [vt])
            P32.put(gv)
            vtm.append(vt)
        o_gmlp = []
        for g in range(4):
            p = psget()
            for tc in range(4):
                mm(p.ap[:, tc * 128:(tc + 1) * 128], vtm[tc].ap[:, g * 128:(g + 1) * 128], pm[:, l, PM["ws"] + g * 128:PM["ws"] + (g + 1) * 128], True, True, [vtm[tc], pm_b], [p])
            tmp = P32.get()
            S.op("vector", lambda e, tmp=tmp, p=p, g=g: e.tensor_tensor(out=tmp.ap.rearrange("p (c t) -> p c t", t=128), in0=p.ap.rearrange("p (c t) -> p c t", t=128), in1=pbc[:, l, 1024 + g * 128:1024 + (g + 1) * 128].rearrange("p (c t) -> p c t", c=1).broadcast_to([128, 4, 128]), op=ALU.add), [p.b, pbc_b], [tmp.b])
            og = P16.get()
            tt("vector", og.ap, tmp.ap, u32[g].ap, ALU.mult, [tmp, u32[g]], [og])
            P32.put(tmp, u32[g])
            o_gmlp.append(og)
            tap("o_gmlp_%d:%d" % (l, g), og, True)
        P16.put(*vtm)
        wv, wb = wload(winv[:, :, 2816:3328], [KC, 512], sb["win", l])
        for c in range(4):
            p = inproj_fm(wv, wb, c)
            cp("gpsimd", zxb[:, c, 0:3], halo[:, l, c, :], [halo_b[l][c]], [zxb_b[c]])
            act(zxb[:, c, 3:515], p.ap, AF.Identity, [p], [zxb_b[c]])
            cp("gpsimd", halo[:, l, c, :], zxb[:, c, 512:515], [zxb_b[c]], [halo_b[l][c]])
        wv, wb = wload(winv[:, :, 3328:3840], [KC, 512], sb["win", l])
        o_lru = []
        for c in range(4):
            p = inproj_fm(wv, wb, c)
            y32 = P32.get()
            act(y32.ap, p.ap, AF.Gelu_apprx_tanh, [p], [y32])
            xc = P32.get()
            cw = lambda j: pv[:, l, PV["cw"] + j * 4 + c:PV["cw"] + j * 4 + c + 1]
            ts(xc.ap, zxb[:, c, 3:515], cw(3), pcol(l, "cb", c), ALU.mult, ALU.add, [zxb_b[c], pv_b], [xc])
            for j in range(3):
                stt(xc.ap, zxb[:, c, j:j + 512], cw(j), xc.ap, ALU.mult, ALU.add, [zxb_b[c], pv_b, xc], [xc])
            xcb = P16.get()
            cp("vector", xcb.ap, xc.ap, [xc], [xcb])
            pr, pi = psget(), psget()
            mm(pr.ap, pm[:, l, PM["lwa"] + c * 128:PM["lwa"] + (c + 1) * 128], xcb.ap, True, True, [pm_b, xcb], [pr])
            mm(pi.ap, pm[:, l, PM["lwx"] + c * 128:PM["lwx"] + (c + 1) * 128], xcb.ap, True, True, [pm_b, xcb], [pi])
            P16.put(xcb)
            rg, ig, aa = P32.get(), P32.get(), P32.get()
            act(rg.ap, pr.ap, AF.Sigmoid, [pr, pv_b], [rg], bias=pcol(l, "ba", c))
            act(ig.ap, pi.ap, AF.Sigmoid, [pi, pv_b], [ig], bias=pcol(l, "bx", c))
            act(aa.ap, rg.ap, AF.Exp, [rg, pv_b], [aa], scale=pcol(l, "clam", c))
            act(rg.ap, rg.ap, AF.Exp, [rg, pv_b], [rg], scale=pcol(l, "clam2", c))
            act(rg.ap, rg.ap, AF.Ln, [rg], [rg], bias=1.0, scale=-1.0)
            act(rg.ap, rg.ap, AF.Exp, [rg], [rg], scale=0.5)
            tt("gpsimd", ig.ap, ig.ap, xc.ap, ALU.mult, [ig, xc], [ig])
            tt("vector", ig.ap, ig.ap, rg.ap, ALU.mult, [ig, rg], [ig])
            hh = xc
            S.op("vector", lambda e, hh=hh, aa=aa, ig=ig, c=c: e.tensor_tensor_scan(out=hh.ap, data0=aa.ap, data1=ig.ap, initial=hprev[:, l, c:c + 1], op0=ALU.mult, op1=ALU.add), [aa.b, ig.b, hprev_b[l][c]], [hh.b])
            cp("gpsimd", hprev[:, l, c:c + 1], hh.ap[:, T - 1:T], [hh], [hprev_b[l][c]])
            ol = P16.get()
            tt("vector", ol.ap, hh.ap, y32.ap, ALU.mult, [hh, y32], [ol])
            P32.put(rg, ig, aa, xc, y32)
            o_lru.append(ol)
            tap("o_lru_%d:%d" % (l, c), ol, True)
        branches = [o_rwkv, o_gmlp, o_lru]
        merged = [P32.get() for _ in range(KC)]
        for b in range(3):
            pbv, pbb = wload(pb_s[l, b].rearrange("(k p) n -> p k n", p=128), [4, D], sb["pb", l, b])
            for half in range(2):
                c0 = OFF_GATE + b * D + half * 512
                wv, wb = wload(winv[:, :, c0:c0 + 512], [KC, 512], sb["win", l])
                for jj in range(4):
                    j = half * 4 + jj
                    pg = inproj_fm(wv, wb, jj)
                    pp = psget()
                    for k in range(4):
                        mm(pp.ap, pbv[:, k, j * 128:(j + 1) * 128], branches[b][k].ap, k == 0, k == 3, [pbb, branches[b][k]], [pp])
                    sg_ = P32.get()
                    act(sg_.ap, pg.ap, AF.Sigmoid, [pg, pv_b], [sg_], bias=pcol(l, "gate_b", b * 8 + j))
                    if b == 0:
                        tt("vector", merged[j].ap, sg_.ap, pp.ap, ALU.mult, [sg_, pp], [merged[j]])
                    else:
                        tt("vector", sg_.ap, sg_.ap, pp.ap, ALU.mult, [sg_, pp], [sg_])
                        tt("gpsimd", merged[j].ap, merged[j].ap, sg_.ap, ALU.add, [merged[j], sg_], [merged[j]])
                    P32.put(sg_)
            P16.put(*branches[b])
        for j in range(KC):
            tap("merged_%d:%d" % (l, j), merged[j])
        mbf = []
        for j in range(KC):
            t_ = P16.get()
            cp("vector", t_.ap, merged[j].ap, [merged[j]], [t_])
            mbf.append(t_)
        P32.put(*merged)
        wov = wo_s[l].rearrange("(k p) n -> p k n", p=128)
        held = {}

        def contrib(c):
            half, jj = divmod(c, 4)
            if half not in held:
                held[half] = wload(wov[:, :, half * 512:(half + 1) * 512], [KC, 512], sb["wo", l])
            wv, wb = held[half]
            p = psget()
            for k in range(KC):
                mm(p.ap, wv[:, k, jj * 128:(jj + 1) * 128], mbf[k].ap, k == 0, k == KC - 1, [wb, mbf[k]], [p])
            return p

        for c in range(KC):
            p = contrib(c)
            stt(x32[c].ap, p.ap, C_MIX, x32[c].ap, ALU.mult, ALU.add, [p, x32[c]], [x32[c]])
        P16.put(*mbf)
        layer_norm(l, 1, x32, xbf, lambda c: None, C_MIX)

    x32 = [P32.get() for _ in range(KC)]
    xbf = [P16.get() for _ in range(KC)]
    out_bufs = []

    def load_x(ti):
        tiles = []
        for tc in range(4):
            for hf in range(2):
                t_ = P32.get()
                dma("sync", t_.ap, x_d[ti * T + tc * 128:ti * T + (tc + 1) * 128, hf * 512:(hf + 1) * 512], [], [t_], "xin%d" % (tc * 2 + hf))
                tiles.append(t_)
        return tiles

    import os
    LIM = int(os.environ.get("KLIM", "99"))

    class StopBuild(Exception):
        pass

    def ckpt(n):
        if n > LIM:
            raise StopBuild()

    try:
      ckpt(1)
      xin = load_x(0)
      for ti in range(NT):
            hl = []
            for t_ in xin:
                hi, lo = P16.get(), P16.get()
                cp("vector", hi.ap, t_.ap, [t_], [hi])
                tt("vector", lo.ap, t_.ap, hi.ap, ALU.subtract, [t_, hi], [lo])
                hl.append((hi, lo))
            for k in range(KC):
                p = psget()
                for tc in range(4):
                    hi, lo = hl[tc * 2 + k // 4]
                    csl = slice((k % 4) * 128, (k % 4 + 1) * 128)
                    mm(p.ap[:, tc * 128:(tc + 1) * 128], hi.ap[:, csl], identb, True, False, [hi, cbf_b], [p])
                    mm(p.ap[:, tc * 128:(tc + 1) * 128], lo.ap[:, csl], identb, False, True, [lo, cbf_b], [p])
                act(x32[k].ap, p.ap, AF.Identity, [p], [x32[k]])
                cp("vector", xbf[k].ap, p.ap, [p], [xbf[k]])
            for hi, lo in hl:
                P16.put(hi, lo)
            P32.put(*xin)
            ckpt(2)
            vfirst = []
            for l in range(2):
                ckpt(3)
                tap_on[0] = (ti == 0)
                ffn(l, 0, x32, xbf)
                for k in range(KC):
                    tap("x_%d_f0:%d" % (l, k), x32[k])
                ckpt(4)
                mixer(l, x32, xbf, vfirst)
                for k in range(KC):
                    tap("x_%d_mix:%d" % (l, k), x32[k])
                ckpt(5)
                if l == 1 and ti + 1 < NT:
                    xin = load_x(ti + 1)
                ffn(l, 1, x32, xbf)
                for k in range(KC):
                    tap("x_%d_f1:%d" % (l, k), x32[k])
            hl = []
            for k in range(KC):
                hi, lo = P16.get(), P16.get()
                cp("vector", hi.ap, x32[k].ap, [x32[k]], [hi])
                tt("vector", lo.ap, x32[k].ap, hi.ap, ALU.subtract, [x32[k], hi], [lo])
                hl.append((hi, lo))
            for tc in range(4):
                for hf in range(2):
                    p = psget()
                    for kk_ in range(4):
                        hi, lo = hl[hf * 4 + kk_]
                        mm(p.ap[:, kk_ * 128:(kk_ + 1) * 128], hi.ap[:, tc * 128:(tc + 1) * 128], identb, True, False, [hi, cbf_b], [p])
                        mm(p.ap[:, kk_ * 128:(kk_ + 1) * 128], lo.ap[:, tc * 128:(tc + 1) * 128], identb, False, True, [lo, cbf_b], [p])
                    t_ = P32.get()
                    act(t_.ap, p.ap, AF.Identity, [p], [t_])
                    ob = Buf()
                    dma("sync", out_d[ti * T + tc * 128:ti * T + (tc + 1) * 128, hf * 512:(hf + 1) * 512], t_.ap, [t_], [ob], "out%d" % (tc * 2 + hf))
                    out_bufs.append(ob)
                    P32.put(t_)
            for hi, lo in hl:
                P16.put(hi, lo)

    except StopBuild:
        pass
    out_bufs = out_bufs + tap_bufs
    if LIM < 99:
        out_bufs = out_bufs + list(cast_last.values())
    cnt = S.emit(final_bufs=out_bufs)
    print("SEMS", {k: v for k, v in cnt.items() if k[0] == "eng"}, len(cnt), "nops", len(S.ops))
    es.close()
    return nc


_CACHE = {}
TAPS = []
LAST_DBG = None


def run(inputs, SEQ, n_cores, trace=False):
    pv, pm, pb = host_params(inputs)
    cs = host_consts()
    if SEQ not in _CACHE:
        _CACHE[SEQ] = build(SEQ)
    nc = _CACHE[SEQ]
    f = lambda k: np.ascontiguousarray(np.asarray(inputs[k], np.float32))
    shared = {"ffn_w1": f("ffn_w1"), "ffn_w3": f("ffn_w3"), "ffn_w2": f("ffn_w2"), "w_in": f("w_in"),
              "p_branch": f("p_branch"), "w_out": f("w_out"), "pvec": pv, "pmat": pm, "pbc": pb, "cst": cs}
    x = np.asarray(inputs["x"], np.float32)
    in_maps = [dict(shared, x=np.ascontiguousarray(x[c, :SEQ])) for c in range(n_cores)]
    res = run_bass_kernel_spmd(nc, in_maps, core_ids=list(range(n_cores)), **({"trace": True} if trace else {}))
    if trace:
        print("EXEC_NS", res.exec_time_ns)
    global LAST_DBG
    LAST_DBG = np.asarray(res.results[0]["dbg"]) if "dbg" in res.results[0] else None
    return np.stack([np.asarray(r["out"]) for r in res.results], axis=0)


def kernel(**inputs):
    return run(inputs, 8192, 8).astype(np.float32)
```
